# Optimizing a Trainium2 kernel written in Bass

```python
import math
import jax, jax.numpy as jnp
from jax import lax
import numpy as np

D_MODEL = 1024
BATCH = 8
SEQ = 2048
DEPTH = 4

CTX_LEN = 256
GRID_W = 64
MLA_HEADS = 8
MLA_NOPE = 64
MLA_ROPE = 32
MLA_V = 64
MLA_QK = MLA_NOPE + MLA_ROPE
Q_LORA = 768
KV_LORA = 256
NA_HEADS = 8
NA_DIM = 64
NA_KH = 8
NA_KW = 16
D_FF = 4 * D_MODEL
N_BRANCH = 2
ROPE_BASE = 10000.0
ROPE_PAIRS = MLA_ROPE // 4
EPS = 1e-6
Q_BLOCK = 128
OFF_CQ = N_BRANCH * D_MODEL
OFF_CKV = OFF_CQ + Q_LORA
OFF_KR = OFF_CKV + KV_LORA
OFF_NA = OFF_KR + MLA_ROPE
IN_COLS = OFF_NA + 3 * NA_HEADS * NA_DIM

kernel_name = "hybrid_mla_natten_dit_trunk"


def rms_norm(x, g):
    xf = x.astype(jnp.float32)
    y = xf * lax.rsqrt(jnp.mean(xf * xf, axis=-1, keepdims=True) + EPS)
    return (y * g.astype(jnp.float32)).astype(x.dtype)


def modulate(h, shift, scale):
    return h * (1 + scale) + shift


def rope_half(x, ang):
    m = x.shape[-1] // 2
    x1, x2 = x[..., :m], x[..., m:]
    cos = jnp.cos(ang).astype(x.dtype)
    sin = jnp.sin(ang).astype(x.dtype)
    return jnp.concatenate([x1 * cos - x2 * sin, x1 * sin + x2 * cos], axis=-1)


def rope_2d(x, ang_r, ang_c):
    half = x.shape[-1] // 2
    return jnp.concatenate([rope_half(x[..., :half], ang_r), rope_half(x[..., half:], ang_c)], axis=-1)


def mixer_proj(h, w_in, g_qa, w_uq, g_kva, w_ukv, g_mq, g_mk, g_nq, g_nk):
    b_, s_ = h.shape[0], h.shape[1]
    p = h @ w_in
    gate_logits = p[..., :OFF_CQ]
    c_q = p[..., OFF_CQ:OFF_CKV]
    c_kv = p[..., OFF_CKV:OFF_KR]
    k_r = p[..., OFF_KR:OFF_NA]
    na = p[..., OFF_NA:].reshape(b_, s_, 3, NA_HEADS, NA_DIM)
    mq = (rms_norm(c_q, g_qa) @ w_uq).reshape(b_, s_, MLA_HEADS, MLA_QK)
    kv = (rms_norm(c_kv, g_kva) @ w_ukv).reshape(b_, s_, MLA_HEADS, MLA_NOPE + MLA_V)
    k_nope, mv = kv[..., :MLA_NOPE], kv[..., MLA_NOPE:]
    k_rope = jnp.broadcast_to(k_r[:, :, None, :], (b_, s_, MLA_HEADS, MLA_ROPE))
    mk = jnp.concatenate([k_nope, k_rope], axis=-1)
    mq = rms_norm(mq, g_mq)
    mk = rms_norm(mk, g_mk)
    nq = rms_norm(na[:, :, 0], g_nq)
    nk = rms_norm(na[:, :, 1], g_nk)
    nv = na[:, :, 2]
    return gate_logits, mq, mk, mv, nq, nk, nv


def dense_attention(q, k, v):
    scale = 1.0 / math.sqrt(q.shape[-1])
    s = jnp.einsum("bqhd,bkhd->bhqk", q, k).astype(jnp.float32) * scale
    p = jax.nn.softmax(s, axis=-1).astype(v.dtype)
    return jnp.einsum("bhqk,bkhe->bqhe", p, v)


def blocked_attention(q, k, v):
    b_, s_, h_, d_ = q.shape
    nb = s_ // Q_BLOCK
    qb = jnp.moveaxis(q.reshape(b_, nb, Q_BLOCK, h_, d_), 1, 0)
    ob = lax.map(lambda blk: dense_attention(blk, k, v), qb)
    return jnp.moveaxis(ob, 0, 1).reshape(b_, s_, h_, v.shape[-1])


def neighborhood_attention(q, k, v, k_ctx, v_ctx, rpb):
    b_, s_, h_, d_ = q.shape
    rows = s_ // GRID_W
    kh = min(NA_KH, rows)
    kw = NA_KW
    scale = 1.0 / math.sqrt(d_)
    to_grid = lambda t: t.reshape(b_, rows, GRID_W, h_, d_).transpose(1, 0, 3, 2, 4)
    q_g, k_g, v_g = to_grid(q), to_grid(k), to_grid(v)
    cols = jnp.arange(GRID_W)
    c_start = jnp.clip(cols - kw // 2, 0, GRID_W - kw)
    col_idx = c_start[:, None] + jnp.arange(kw)[None, :]
    dc_idx = col_idx - cols[:, None] + (NA_KW - 1)
    rpb_cols = rpb[:, :, dc_idx]

    def row_fn(args):
        q_r, r = args
        r_start = jnp.clip(r - kh // 2, 0, rows - kh)
        k_rows = lax.dynamic_slice_in_dim(k_g, r_start, kh, axis=0)
        v_rows = lax.dynamic_slice_in_dim(v_g, r_start, kh, axis=0)
        k_win = k_rows[:, :, :, col_idx]
        v_win = v_rows[:, :, :, col_idx]
        dr_idx = r_start + jnp.arange(kh) - r + (NA_KH - 1)
        bias = jnp.transpose(jnp.take(rpb_cols, dr_idx, axis=1), (0, 2, 1, 3))
        s_win = jnp.einsum("bhqd,ibhqjd->bhqij", q_r, k_win).astype(jnp.float32) * scale
        s_win = (s_win + bias.astype(jnp.float32)).reshape(b_, h_, GRID_W, kh * kw)
        s_ctx = jnp.einsum("bhqd,bkhd->bhqk", q_r, k_ctx).astype(jnp.float32) * scale
        p = jax.nn.softmax(jnp.concatenate([s_win, s_ctx], axis=-1), axis=-1).astype(v.dtype)
        p_win = p[..., :kh * kw].reshape(b_, h_, GRID_W, kh, kw)
        p_ctx = p[..., kh * kw:]
        return (jnp.einsum("bhqij,ibhqjd->bhqd", p_win, v_win)
                + jnp.einsum("bhqk,bkhd->bhqd", p_ctx, v_ctx))

    o = lax.map(row_fn, (q_g, jnp.arange(rows)))
    return o.transpose(1, 0, 3, 2, 4).reshape(b_, s_, h_ * d_)


def gated_merge(gate_logits, y_mla, y_na, w_mla_o, w_na_o, w_out):
    g = jax.nn.sigmoid(gate_logits.astype(jnp.float32)).astype(y_mla.dtype)
    y = g[..., :D_MODEL] * (y_mla @ w_mla_o) + g[..., D_MODEL:] * (y_na @ w_na_o)
    return y @ w_out


def squared_relu_mlp(h, w_ff1, w_ff2):
    return jnp.square(jax.nn.relu(h @ w_ff1)) @ w_ff2


def _normal(key, shape, scale):
    return jax.random.normal(key, shape, jnp.float32) * scale


def setup_inputs(seed: int = 0) -> dict:
    key = jax.random.key(seed)
    ks = jax.random.split(key, 24)
    L = DEPTH
    gain = lambda k, n: 1.0 + _normal(k, (L, n), 0.1)
    return {
        "x": _normal(ks[0], (BATCH, SEQ, D_MODEL), 1.0),
        "c": _normal(ks[1], (BATCH, D_MODEL), 1.0),
        "ctx": _normal(ks[2], (BATCH, CTX_LEN, D_MODEL), 1.0),
        "c_ctx": _normal(ks[3], (D_MODEL,), 1.0),
        "w_ada": _normal(ks[4], (L, D_MODEL, 6 * D_MODEL), D_MODEL ** -0.5),
        "b_ada": _normal(ks[5], (L, 6 * D_MODEL), 0.02),
        "g_attn": gain(ks[6], D_MODEL),
        "w_in": _normal(ks[7], (L, D_MODEL, IN_COLS), D_MODEL ** -0.5),
        "g_qa": gain(ks[8], Q_LORA),
        "w_uq": _normal(ks[9], (L, Q_LORA, MLA_HEADS * MLA_QK), Q_LORA ** -0.5),
        "g_kva": gain(ks[10], KV_LORA),
        "w_ukv": _normal(ks[11], (L, KV_LORA, MLA_HEADS * (MLA_NOPE + MLA_V)), KV_LORA ** -0.5),
        "g_mla_q": gain(ks[12], MLA_QK),
        "g_mla_k": gain(ks[13], MLA_QK),
        "g_na_q": gain(ks[14], NA_DIM),
        "g_na_k": gain(ks[15], NA_DIM),
        "rpb": _normal(ks[16], (L, NA_HEADS, 2 * NA_KH - 1, 2 * NA_KW - 1), 0.1),
        "w_mla_o": _normal(ks[17], (L, MLA_HEADS * MLA_V, D_MODEL), (MLA_HEADS * MLA_V) ** -0.5),
        "w_na_o": _normal(ks[18], (L, NA_HEADS * NA_DIM, D_MODEL), (NA_HEADS * NA_DIM) ** -0.5),
        "w_out": _normal(ks[19], (L, D_MODEL, D_MODEL), D_MODEL ** -0.5),
        "g_mlp": gain(ks[20], D_MODEL),
        "w_ff1": _normal(ks[21], (L, D_MODEL, D_FF), D_MODEL ** -0.5),
        "w_ff2": _normal(ks[22], (L, D_FF, D_MODEL), D_FF ** -0.5),
    }


def reference(x, c, ctx, c_ctx, w_ada, b_ada, g_attn, w_in, g_qa, w_uq, g_kva, w_ukv,
              g_mla_q, g_mla_k, g_na_q, g_na_k, rpb, w_mla_o, w_na_o, w_out,
              g_mlp, w_ff1, w_ff2):
    b_, s_, _ = x.shape
    l_ = ctx.shape[1]
    t = jnp.arange(s_)
    inv_freq = ROPE_BASE ** (-jnp.arange(ROPE_PAIRS, dtype=jnp.float32) / ROPE_PAIRS)
    ang_r = ((t // GRID_W).astype(jnp.float32)[:, None] * inv_freq)[:, None, :]
    ang_c = ((t % GRID_W).astype(jnp.float32)[:, None] * inv_freq)[:, None, :]
    silu_c = jax.nn.silu(c)
    silu_ctx = jax.nn.silu(c_ctx)
    x_lat, x_ctx = x, ctx
    for l in range(DEPTH):
        last = l == DEPTH - 1
        mod_lat = (silu_c @ w_ada[l] + b_ada[l])[:, None, :]
        mod_ctx = silu_ctx @ w_ada[l] + b_ada[l]
        sh1, sc1, gt1, sh2, sc2, gt2 = jnp.split(mod_lat, 6, axis=-1)
        csh1, csc1, cgt1, csh2, csc2, cgt2 = jnp.split(mod_ctx, 6, axis=-1)
        proj_w = (w_in[l], g_qa[l], w_uq[l], g_kva[l], w_ukv[l],
                  g_mla_q[l], g_mla_k[l], g_na_q[l], g_na_k[l])

        h_c = modulate(rms_norm(x_ctx, g_attn[l]), csh1, csc1)
        gl_c, mq_c, mk_c, mv_c, nq_c, nk_c, nv_c = mixer_proj(h_c, *proj_w)

        h = modulate(rms_norm(x_lat, g_attn[l]), sh1, sc1)
        gl, mq, mk, mv, nq, nk, nv = mixer_proj(h, *proj_w)
        mq = jnp.concatenate([mq[..., :MLA_NOPE], rope_2d(mq[..., MLA_NOPE:], ang_r, ang_c)], axis=-1)
        mk = jnp.concatenate([mk[..., :MLA_NOPE], rope_2d(mk[..., MLA_NOPE:], ang_r, ang_c)], axis=-1)
        y_mla = blocked_attention(mq, jnp.concatenate([mk_c, mk], axis=1),
                                  jnp.concatenate([mv_c, mv], axis=1)).reshape(b_, s_, MLA_HEADS * MLA_V)
        y_na = neighborhood_attention(nq, nk, nv, nk_c, nv_c, rpb[l])
        x_lat = x_lat + gt1 * gated_merge(gl, y_mla, y_na, w_mla_o[l], w_na_o[l], w_out[l])

        if not last:
            y_mla_c = dense_attention(mq_c, mk_c, mv_c).reshape(b_, l_, MLA_HEADS * MLA_V)
            y_na_c = dense_attention(nq_c, nk_c, nv_c).reshape(b_, l_, NA_HEADS * NA_DIM)
            x_ctx = x_ctx + cgt1 * gated_merge(gl_c, y_mla_c, y_na_c, w_mla_o[l], w_na_o[l], w_out[l])
            h2_c = modulate(rms_norm(x_ctx, g_mlp[l]), csh2, csc2)
            x_ctx = x_ctx + cgt2 * squared_relu_mlp(h2_c, w_ff1[l], w_ff2[l])

        h2 = modulate(rms_norm(x_lat, g_mlp[l]), sh2, sc2)
        x_lat = x_lat + gt2 * squared_relu_mlp(h2, w_ff1[l], w_ff2[l])
    return x_lat
```

```python
import math
from contextlib import ExitStack

import numpy as np
import concourse.bass as bass
import concourse.mybir as mybir
from concourse.bass_utils import run_bass_kernel_spmd

F32 = mybir.dt.float32
BF16 = mybir.dt.bfloat16
ALU = mybir.AluOpType
AF = mybir.ActivationFunctionType

ENGS = ("pe", "act", "dve", "pool", "sp")
BAR_ENGS = ("pe", "act", "dve", "sp")

D = 1024
SEQ = 2048
CTX = 256
T = CTX + SEQ
NKT = T // 128
DEPTH = 4
IN_COLS = 4640
EPS = 1e-6
NEG = -30000.0


class Tile:
    __slots__ = ("name", "w", "rs", "ap")

    def __init__(self, name, ap=None):
        self.name = name
        self.w = None
        self.rs = {}
        self.ap = ap

    def __getitem__(self, k):
        return self.ap[k]


class Op:
    __slots__ = ("eng", "fn", "deps", "needs_inc", "seq", "dma", "dtok", "bar")

    def __init__(self, eng, fn):
        self.eng = eng
        self.fn = fn
        self.deps = []
        self.needs_inc = False
        self.seq = 0
        self.dma = False
        self.dtok = None
        self.bar = True


class Prog:
    def __init__(self, nc):
        self.nc = nc
        self.ops = {e: [] for e in ENGS}
        self.dma_cnt = {}
        self.dma_last = {}
        self.last = {}

    def _add_dep(self, o, d):
        if d is None or d is o:
            return
        if (not d.dma) and (not o.dma) and d.eng == "pe" and o.eng == "pe":
            return
        if d not in o.deps:
            o.deps.append(d)

    def op(self, eng, fn, reads=(), writes=(), dma_key=None, bar=True):
        o = Op(eng, fn)
        if dma_key is not None:
            o.dma = True
            o.bar = bar
            n = self.dma_cnt.get(dma_key, 0) + 1
            self.dma_cnt[dma_key] = n
            o.dtok = (dma_key, 16 * n)
            if bar:
                self.dma_last[dma_key] = o
        else:
            self.last[eng] = o
        for t in reads:
            self._add_dep(o, t.w)
        for t in writes:
            self._add_dep(o, t.w)
            for r in t.rs.values():
                self._add_dep(o, r)
        for d in o.deps:
            if not d.dma:
                d.needs_inc = True
        rk = ("dma", dma_key) if o.dma else eng
        for t in reads:
            t.rs[rk] = o
        for t in writes:
            t.w = o
            t.rs = {}
        self.ops[eng].append(o)
        return o

    def barrier(self):
        lasts = [self.last[e] for e in BAR_ENGS if e in self.last]
        dm = list(self.dma_last.values())
        self.dma_last = {}
        for e in BAR_ENGS:
            o = Op(e, None)
            for d in lasts:
                if d.eng != e:
                    o.deps.append(d)
                    d.needs_inc = True
            for d in dm:
                o.deps.append(d)
            self.ops[e].append(o)

    def emit(self):
        nc = self.nc
        for e in ENGS:
            c = 0
            for o in self.ops[e]:
                if o.needs_inc:
                    c += 1
                    o.seq = c
        with ExitStack() as es:
            esem = {e: es.enter_context(nc.semaphore("s_" + e)) for e in ENGS}
            dsem = {}
            for i, k in enumerate(self.dma_cnt):
                dsem[k] = es.enter_context(nc.semaphore("d%d" % i))
            block = es.enter_context(nc.Block())

            def run(e, engobj):
                known = {}
                for o in self.ops[e]:
                    for d in o.deps:
                        if d.dma:
                            sk, val = d.dtok
                            sem = dsem[sk]
                            kk = ("d", sk)
                        else:
                            sem = esem[d.eng]
                            val = d.seq
                            kk = d.eng
                        if known.get(kk, 0) >= val:
                            continue
                        known[kk] = val
                        engobj.wait_ge(sem, val)
                    if o.fn is None:
                        continue
                    ins = o.fn(engobj)
                    if o.dma:
                        ins.then_inc(dsem[o.dtok[0]], 16)
                    elif o.needs_inc:
                        ins.then_inc(esem[e], 1)

            @block.tensor
            def _(eng):
                run("pe", eng)

            @block.scalar
            def _(eng):
                run("act", eng)

            @block.vector
            def _(eng):
                run("dve", eng)

            @block.gpsimd
            def _(eng):
                run("pool", eng)

            @block.sync
            def _(eng):
                run("sp", eng)


def token_blocks(include_ctx=True):
    bl = []
    if include_ctx:
        bl.append((0, CTX, 1))
    for i in range(4):
        bl.append((CTX + 512 * i, 512, 0))
    return bl


def na_valid_rows(kr, R):
    rows = [qr for qr in range(R, R + 8) if min(max(qr - 4, 0), 24) <= kr < min(max(qr - 4, 0), 24) + 8]
    if not rows:
        return None
    assert rows == list(range(rows[0], rows[-1] + 1))
    return rows[0], rows[-1] + 1


def build_program(depth=DEPTH, debug=False):
    nc = bass.Bass("TRN2", target_bir_lowering=False)
    L = depth

    def din(name, shape, dt=F32):
        return nc.dram_tensor(name, list(shape), dt, kind="ExternalInput").ap()

    skind = "ExternalOutput" if debug else "Internal"

    def dscr(name, shape, dt=BF16):
        return nc.dram_tensor(name, list(shape), dt, kind=skind).ap()

    xT_d = din("xT", [D, T])
    cT_d = din("cT", [128, 16])
    w_ada_d = din("w_ada", [L, D, 6 * D])
    b_ada_d = din("b_adaT", [L, 128, 48])
    g_attn_d = din("g_attnT", [L, 128, 8])
    w_in_d = din("w_in", [L, D, IN_COLS])
    g_qa_d = din("g_qaT", [L, 128, 6])
    w_uq_d = din("w_uq", [L, 768, 768])
    g_kva_d = din("g_kvaT", [L, 128, 2])
    w_ukv_d = din("w_ukv", [L, 256, 1024])
    g_mq_d = din("g_mq", [L, 96, 1])
    g_mk_d = din("g_mk", [L, 96, 1])
    g_nq_d = din("g_nq2", [L, 128, 1])
    g_nk_d = din("g_nk2", [L, 128, 1])
    rpbT_d = din("rpbT", [L, 31, 120])
    w_mo_d = din("w_mla_o", [L, 512, D])
    w_no_d = din("w_na_o", [L, 512, D])
    w_out_d = din("w_out", [L, D, D])
    g_mlp_d = din("g_mlpT", [L, 128, 8])
    w_ff1_d = din("w_ff1", [L, D, 4 * D])
    w_ff2_d = din("w_ff2", [L, 4 * D, D])
    cos_d = din("cosT", [96, T])
    sin_d = din("sinT", [96, T])
    rt_d = din("rotT", [96, 96])
    dpad_d = din("dpad", [31, 127])
    cmask_d = din("colmask", [64, 64])
    ident_d = din("ident", [128, 128])
    bdiag_d = din("bdiag", [128, 128])
    outT_d = nc.dram_tensor("outT", [D, SEQ], F32, kind="ExternalOutput").ap()

    QM_d = dscr("QM", [8, 96, T])
    KM_d = dscr("KM", [8, 96, T])
    VM_d = dscr("VM", [8, 128, NKT, 65])
    QN_d = dscr("QN", [8, 64, T])
    KN_d = dscr("KN", [8, 64, T])
    VN_d = dscr("VN", [8, 128, NKT, 65])
    YT_d = dscr("YT", [16, 64, T])
    xdbg_d = dscr("xdbg", [2, D, T], F32) if debug else None

    es = ExitStack()
    with es:
        def sb(name, shape, dt):
            return es.enter_context(nc.sbuf_tensor(name, list(shape), dt))

        P = Prog(nc)

        def ptile(name, shape, dt):
            return Tile(name, sb(name, shape, dt))

        xT = ptile("xTs", [128, 8, T], F32)
        ring = [ptile("ring%d" % i, [128, 8192], BF16) for i in range(4)]
        cosT = ptile("cosTs", [96, T], BF16)
        sinT = ptile("sinTs", [96, T], BF16)
        ident = ptile("idents", [128, 128], BF16)
        ones = ptile("oness", [128, 128], BF16)
        bdiag = ptile("bdiags", [128, 128], BF16)
        rotT = ptile("rotTs", [96, 96], BF16)
        dpad = ptile("dpads", [31, 127], BF16)
        cmask = ptile("cmasks", [64, 64], F32)
        cTs = ptile("cTs", [128, 16], F32)
        siluT = ptile("siluTs", [128, 8, 2], BF16)
        mod = ptile("mods", [128, 48, 2], F32)
        A1 = ptile("A1s", [128, 8, 2], F32)
        A2 = ptile("A2s", [128, 8, 2], F32)
        b_ada = ptile("b_adas", [128, 48], F32)
        g_attn = ptile("g_attns", [128, 8], F32)
        g_mlp = ptile("g_mlps", [128, 8], F32)
        g_qa = ptile("g_qas", [128, 6], F32)
        g_kva = ptile("g_kvas", [128, 2], F32)
        g_mq = ptile("g_mqs", [96, 1], F32)
        g_mk = ptile("g_mks", [96, 1], F32)
        g_nq = ptile("g_nqs", [128, 1], F32)
        g_nk = ptile("g_nks", [128, 1], F32)
        wkr = ptile("wkrs", [128, 8, 96], BF16)
        wkpad = ptile("wkpads", [128, 2, 8, 96], BF16)
        rpbT = ptile("rpbTs", [31, 120], BF16)
        ARB = 20480
        ARF = 2048
        arena_b = sb("arena_b", [128, ARB], BF16)
        arena_f = sb("arena_f", [128, ARF], F32)
        psb = [Tile("ps%d" % i, es.enter_context(nc.psum_tensor("ps%d" % i, [128, 512], F32))) for i in range(8)]

        dt_ = {n: Tile(n) for n in ("QM", "KM", "VM", "QN", "KN", "VN", "YT", "out", "xdbg")}

        class Arena:
            def __init__(self):
                self.ob = 0
                self.of = 0
                self.n = 0

            def reset(self):
                self.ob = 0
                self.of = 0

            def b(self, name, parts, shape):
                n = int(np.prod(shape))
                ap = arena_b[0:parts, self.ob:self.ob + n]
                self.ob += n + (n % 2)
                assert self.ob <= ARB, (name, self.ob)
                return Tile(name, shp(ap, shape))

            def f(self, name, parts, shape):
                n = int(np.prod(shape))
                ap = arena_f[0:parts, self.of:self.of + n]
                self.of += n
                assert self.of <= ARF, (name, self.of)
                return Tile(name, shp(ap, shape))

        def shp(ap, shape):
            if len(shape) == 1:
                return ap
            if len(shape) == 2:
                return ap.rearrange("p (a b) -> p a b", a=shape[0])
            if len(shape) == 3:
                return ap.rearrange("p (a b c) -> p a b c", a=shape[0], b=shape[1])
            raise ValueError

        AR = Arena()

        psc = [0]

        def psum(k=None):
            i = psc[0] % 8 if k is None else k
            if k is None:
                psc[0] += 1
            return psb[i]

        def mm(out, lhsT, rhs, start, stop, reads, writes, **kw):
            return P.op("pe", lambda e: e.matmul(out, lhsT, rhs, start=start, stop=stop, **kw), reads, writes)

        def act(out, in_, func, reads, writes, **kw):
            return P.op("act", lambda e: e.activation(out, in_, func, **kw), reads, writes)

        def tt(out, a, b, op, reads, writes, eng="dve"):
            return P.op(eng, lambda e: e.tensor_tensor(out, a, b, op), reads, writes)

        def stt(out, in0, scalar, in1, op0, op1, reads, writes, eng="dve"):
            return P.op(eng, lambda e: e.scalar_tensor_tensor(out, in0, scalar, in1, op0, op1), reads, writes)

        def ts(out, in0, s1, s2, op0, op1, reads, writes, eng="dve"):
            if s2 is None:
                return P.op(eng, lambda e: e.tensor_scalar(out, in0, s1, None, op0), reads, writes)
            return P.op(eng, lambda e: e.tensor_scalar(out, in0, s1, s2, op0, op1), reads, writes)

        def cp(out, in_, reads, writes, eng="dve"):
            return P.op(eng, lambda e: e.tensor_copy(out, in_), reads, writes)

        def recip(out, in_, reads, writes):
            return P.op("dve", lambda e: e.reciprocal(out, in_), reads, writes)

        def memset(ap, v, writes, eng="dve"):
            return P.op(eng, lambda e: e.memset(ap, v), (), writes)

        def dma(eng, out, in_, reads, writes, key, bar=True):
            return P.op(eng, lambda e: e.dma_start(out=out, in_=in_), reads, writes, dma_key=key, bar=bar)

        def wload(slot, out_ap, in_ap):
            return dma("pool", out_ap, in_ap, (), [ring[slot]], "ring%d" % slot, bar=False)

        def sload(tile, out_ap, in_ap, eng="sp"):
            return dma(eng, out_ap, in_ap, (), [tile], "t_" + tile.name, bar=False)

        def rslot(slot, k, n):
            return ring[slot].ap[:, 0:k * n].rearrange("p (k n) -> p k n", k=k)

        evq = [0]

        def evac(out, in_, reads, writes):
            evq[0] += 1
            if evq[0] % 2:
                return act(out, in_, AF.Copy, reads, writes)
            return cp(out, in_, reads, writes)

        dma("sp", xT.ap[:, :, :], xT_d.rearrange("(k p) t -> p k t", p=128), (), [xT], "xin", bar=False)
        sload(cTs, cTs.ap[:, :], cT_d[:, :])
        sload(cmask, cmask.ap[:, :], cmask_d[:, :])
        for tl, d_ in ((cosT, cos_d), (sinT, sin_d), (ident, ident_d), (bdiag, bdiag_d), (rotT, rt_d), (dpad, dpad_d)):
            sload(tl, tl.ap[:, :], d_[:, :], eng="pool")
        memset(ones.ap[:, :], 1.0, [ones])
        memset(wkr.ap[:, :, :], 0.0, [wkr])
        memset(wkpad.ap[:, :, :, :], 0.0, [wkpad])
        act(siluT.ap[:, :, :].rearrange("p k c -> p (k c)"), cTs.ap[:, :], AF.Silu, [cTs], [siluT])

        def rms_rstd(sq_chunks, sq_tiles, nfeat, parts, w, rt, rstd, lhs_ones, lhs_tile):
            pt = psum()
            n = len(sq_chunks)
            for i, s in enumerate(sq_chunks):
                mm(pt.ap[0:parts, 0:w], lhs_ones, s, i == 0, i == n - 1, [lhs_tile] + sq_tiles, [pt])
            act(rt.ap[0:parts, 0:w], pt.ap[0:parts, 0:w], AF.Sqrt, [pt], [rt], scale=1.0 / nfeat, bias=EPS)
            recip(rstd.ap[0:parts, 0:w], rt.ap[0:parts, 0:w], [rt], [rstd])

        def norm_mod(hT, t0, w, col, Aa, sh_idx, rt, rstd, tmp):
            act(hT.ap[:, :, 0:w], xT.ap[:, :, t0:t0 + w], AF.Square, [xT], [hT])
            rms_rstd([hT.ap[:, kc, 0:w] for kc in range(8)], [hT], float(D), 128, w, rt, rstd, ones.ap[:, :], ones)
            for kc in range(8):
                tm = tmp[kc % len(tmp)]
                stt(tm.ap[:, 0:w], xT.ap[:, kc, t0:t0 + w], Aa.ap[:, kc, col:col + 1], rstd.ap[:, 0:w],
                    ALU.mult, ALU.mult, [xT, Aa, rstd], [tm])
                act(hT.ap[:, kc, 0:w], tm.ap[:, 0:w], AF.Identity, [tm, mod], [hT],
                    bias=mod.ap[:, sh_idx + kc, col:col + 1], scale=1.0)

        for l in range(L):
            last = (l == L - 1)
            sload(b_ada, b_ada.ap[:, :], b_ada_d[l])
            sload(g_attn, g_attn.ap[:, :], g_attn_d[l])
            sload(g_mlp, g_mlp.ap[:, :], g_mlp_d[l])
            sload(g_qa, g_qa.ap[:, :], g_qa_d[l])
            sload(g_kva, g_kva.ap[:, :], g_kva_d[l])
            sload(g_mq, g_mq.ap[:, :], g_mq_d[l])
            sload(g_mk, g_mk.ap[:, :], g_mk_d[l])
            sload(g_nq, g_nq.ap[:, :], g_nq_d[l])
            sload(g_nk, g_nk.ap[:, :], g_nk_d[l])
            sload(rpbT, rpbT.ap[:, :], rpbT_d[l], eng="pool")
            pmod = psum()
            w_ada_v = w_ada_d[l].rearrange("(k p) n -> p k n", p=128)
            for blk in range(6):
                s = blk % 4
                wload(s, rslot(s, 8, 1024), w_ada_v[:, :, blk * 1024:(blk + 1) * 1024])
                wv = rslot(s, 8, 1024)
                for j in range(8):
                    ch = blk * 8 + j
                    for kc in range(8):
                        mm(pmod.ap[:, ch * 2:ch * 2 + 2], wv[:, kc, j * 128:(j + 1) * 128], siluT.ap[:, kc, :],
                           kc == 0, kc == 7, [ring[s], siluT], [pmod])
            tt(mod.ap[:, :, :], pmod.ap[:, 0:96].rearrange("p (a b) -> p a b", b=2),
               b_ada.ap[:, :].unsqueeze(2).to_broadcast([128, 48, 2]), ALU.add, [pmod, b_ada], [mod])
            stt(A1.ap[:, :, :], mod.ap[:, 8:16, :], 1.0, g_attn.ap[:, :].unsqueeze(2).to_broadcast([128, 8, 2]),
                ALU.add, ALU.mult, [mod, g_attn], [A1])
            stt(A2.ap[:, :, :], mod.ap[:, 32:40, :], 1.0, g_mlp.ap[:, :].unsqueeze(2).to_broadcast([128, 8, 2]),
                ALU.add, ALU.mult, [mod, g_mlp], [A2])
            ts(g_mq.ap[:, :], g_mq.ap[:, :], 96.0 ** -0.5, None, ALU.mult, ALU.bypass, [g_mq], [g_mq])
            ts(g_nq.ap[:, :], g_nq.ap[:, :], 0.125, None, ALU.mult, ALU.bypass, [g_nq], [g_nq])

            w_in_v = w_in_d[l].rearrange("(k p) n -> p k n", p=128)
            wload(0, rslot(0, 8, 1024), w_in_v[:, :, 2048:3072])
            wload(1, rslot(1, 8, 1024), w_in_v[:, :, 3104:4128])
            wload(2, rslot(2, 8, 512), w_in_v[:, :, 4128:4640])
            uqv = ring[3].ap[:, 0:6 * 768].rearrange("p (k n) -> p k n", k=6)
            ukvv = ring[3].ap[:, 6 * 768:6 * 768 + 2048].rearrange("p (k n) -> p k n", k=2)
            wload(3, uqv, w_uq_d[l].rearrange("(k p) n -> p k n", p=128))
            wload(3, ukvv, w_ukv_d[l].rearrange("(k p) n -> p k n", p=128))
            dma("pool", wkr.ap[:, :, 64:96], w_in_v[:, :, 3072:3104], (), [wkr], "t_wkr", bar=False)
            for kc in range(6):
                ts(uqv[:, kc, :], uqv[:, kc, :], g_qa.ap[:, kc:kc + 1], None, ALU.mult, ALU.bypass, [ring[3], g_qa], [ring[3]])
            for kc in range(2):
                ts(ukvv[:, kc, :], ukvv[:, kc, :], g_kva.ap[:, kc:kc + 1], None, ALU.mult, ALU.bypass, [ring[3], g_kva], [ring[3]])
            for kc in range(2):
                cp(wkpad.ap[:, kc, :, 0:64], ukvv[:, kc, :].rearrange("p (h c) -> p h c", h=8)[:, :, 0:64], [ring[3]], [wkpad])
            w0 = rslot(0, 8, 1024)
            w1 = rslot(1, 8, 1024)
            w2 = rslot(2, 8, 512)

            AR.reset()
            hT = AR.b("hT", 128, [8, 512])
            cq = AR.b("cq", 128, [6, 512])
            sq6 = AR.b("sq6", 128, [6, 512])
            ckv = AR.b("ckv", 128, [2, 512])
            krp = AR.b("krp", 96, [512])
            ugs = [AR.b("ug%d" % _i, 128, [512]) for _i in range(2)]
            usqs = [AR.b("usq%d" % _i, 128, [512]) for _i in range(2)]
            qst = [AR.b("qst%d" % _i, 128, [512]) for _i in range(3)]
            Vst = AR.b("Vst", 128, [8, 4, 65])
            Vst2 = AR.b("Vst2", 128, [8, 4, 65])
            rt = AR.f("rt", 128, [512])
            rstd = AR.f("rstd", 128, [512])
            tmpf = [AR.f("tmpf", 128, [512]) for _ in range(2)]
            memset(Vst.ap[:, :, :, 64:65], 1.0, [Vst])
            memset(Vst2.ap[:, :, :, 64:65], 1.0, [Vst2])
            hc = [0]

            def head_finish(pu, dd, w, t0, gain, rope, bd, dst, dst_tile):
                i = hc[0] % 2
                hc[0] += 1
                ug, usq, q_ = ugs[i], usqs[i], qst[hc[0] % 3]
                act(ug.ap[0:dd, 0:w], pu.ap[0:dd, 0:w], AF.Identity, [pu, gain], [ug], scale=gain.ap[0:dd, 0:1], bias=0.0)
                act(usq.ap[0:dd, 0:w], pu.ap[0:dd, 0:w], AF.Square, [pu], [usq])
                if bd:
                    rms_rstd([usq.ap[0:dd, 0:w]], [usq], 64.0, dd, w, rt, rstd, bdiag.ap[0:dd, 0:dd], bdiag)
                else:
                    rms_rstd([usq.ap[0:dd, 0:w]], [usq], float(dd), dd, w, rt, rstd, ones.ap[0:dd, 0:dd], ones)
                if rope:
                    pr = psum()
                    mm(pr.ap[0:dd, 0:w], rotT.ap[0:dd, 0:dd], ug.ap[0:dd, 0:w], True, True, [rotT, ug], [pr])
                    t1, t2 = tmpf
                    tt(t1.ap[0:dd, 0:w], ug.ap[0:dd, 0:w], cosT.ap[0:dd, t0:t0 + w], ALU.mult, [ug, cosT], [t1])
                    tt(t2.ap[0:dd, 0:w], pr.ap[0:dd, 0:w], sinT.ap[0:dd, t0:t0 + w], ALU.mult, [pr, sinT], [t2])
                    tt(t1.ap[0:dd, 0:w], t1.ap[0:dd, 0:w], t2.ap[0:dd, 0:w], ALU.add, [t1, t2], [t1])
                    tt(q_.ap[0:dd, 0:w], t1.ap[0:dd, 0:w], rstd.ap[0:dd, 0:w], ALU.mult, [t1, rstd], [q_])
                else:
                    tt(q_.ap[0:dd, 0:w], ug.ap[0:dd, 0:w], rstd.ap[0:dd, 0:w], ALU.mult, [ug, rstd], [q_])
                dma("sp", dst, q_.ap[0:dd, 0:w], [q_], [dst_tile], "st_" + q_.name)

            for (t0, w, col) in token_blocks(True):
                nt = w // 128
                kt0 = t0 // 128
                norm_mod(hT, t0, w, col, A1, 0, rt, rstd, tmpf)
                for j in range(8):
                    pt = psum()
                    for kc in range(8):
                        mm(pt.ap[:, 0:w], w0[:, kc, j * 128:(j + 1) * 128], hT.ap[:, kc, 0:w], kc == 0, kc == 7, [ring[0], hT], [pt])
                    if j < 6:
                        evac(cq.ap[:, j, 0:w], pt.ap[:, 0:w], [pt], [cq])
                    else:
                        evac(ckv.ap[:, j - 6, 0:w], pt.ap[:, 0:w], [pt], [ckv])
                act(sq6.ap[:, :, 0:w], cq.ap[:, :, 0:w], AF.Square, [cq], [sq6])
                rms_rstd([sq6.ap[:, j, 0:w] for j in range(6)], [sq6], 768.0, 128, w, rt, rstd, ones.ap[:, :], ones)
                tt(cq.ap[:, :, 0:w], cq.ap[:, :, 0:w], rstd.ap[:, 0:w].unsqueeze(1).to_broadcast([128, 6, w]), ALU.mult, [cq, rstd], [cq])
                act(sq6.ap[:, 0:2, 0:w], ckv.ap[:, :, 0:w], AF.Square, [ckv], [sq6])
                rms_rstd([sq6.ap[:, j, 0:w] for j in range(2)], [sq6], 256.0, 128, w, rt, rstd, ones.ap[:, :], ones)
                tt(ckv.ap[:, :, 0:w], ckv.ap[:, :, 0:w], rstd.ap[:, 0:w].unsqueeze(1).to_broadcast([128, 2, w]), ALU.mult, [ckv, rstd], [ckv])
                pt = psum()
                for kc in range(8):
                    mm(pt.ap[0:96, 0:w], wkr.ap[:, kc, :], hT.ap[:, kc, 0:w], kc == 0, kc == 7, [wkr, hT], [pt])
                evac(krp.ap[0:96, 0:w], pt.ap[0:96, 0:w], [pt], [krp])
                for h in range(8):
                    pt = psum()
                    for kc in range(6):
                        mm(pt.ap[0:96, 0:w], uqv[:, kc, h * 96:(h + 1) * 96], cq.ap[:, kc, 0:w], kc == 0, kc == 5, [ring[3], cq], [pt])
                    head_finish(pt, 96, w, t0, g_mq, True, False, QM_d[h, :, t0:t0 + w], dt_["QM"])
                for h in range(8):
                    pt = psum()
                    for kc in range(2):
                        mm(pt.ap[0:96, 0:w], wkpad.ap[:, kc, h, :], ckv.ap[:, kc, 0:w], kc == 0, False, [wkpad, ckv], [pt])
                    mm(pt.ap[0:96, 0:w], ident.ap[0:96, 0:96], krp.ap[0:96, 0:w], False, True, [ident, krp], [pt])
                    head_finish(pt, 96, w, t0, g_mk, True, False, KM_d[h, :, t0:t0 + w], dt_["KM"])
                for i in range(nt):
                    pt = psum()
                    for kc in range(2):
                        mm(pt.ap[:, 0:512].rearrange("p (h c) -> p h c", h=8), ckv.ap[:, kc, i * 128:(i + 1) * 128],
                           ukvv[:, kc, :].rearrange("p (h c) -> p h c", h=8)[:, :, 64:128], kc == 0, kc == 1, [ckv, ring[3]], [pt])
                    evac(Vst.ap[:, :, i, 0:64], pt.ap[:, 0:512].rearrange("p (h c) -> p h c", h=8), [pt], [Vst])
                dma("sp", VM_d[:, :, kt0:kt0 + nt, :].rearrange("h p k e -> p h k e"), Vst.ap[:, :, 0:nt, :], [Vst], [dt_["VM"]], "st_vst")
                for qk in range(2):
                    for jp in range(4):
                        pt = psum()
                        c0 = qk * 512 + jp * 128
                        for kc in range(8):
                            mm(pt.ap[:, 0:w], w1[:, kc, c0:c0 + 128], hT.ap[:, kc, 0:w], kc == 0, kc == 7, [ring[1], hT], [pt])
                        dd_ = (QN_d if qk == 0 else KN_d)[2 * jp:2 * jp + 2, :, t0:t0 + w].rearrange("a d t -> (a d) t")
                        head_finish(pt, 128, w, t0, g_nq if qk == 0 else g_nk, False, True, dd_, dt_["QN" if qk == 0 else "KN"])
                for i in range(nt):
                    pt = psum()
                    for kc in range(8):
                        mm(pt.ap[:, 0:512], hT.ap[:, kc, i * 128:(i + 1) * 128], w2[:, kc, :], kc == 0, kc == 7, [hT, ring[2]], [pt])
                    evac(Vst2.ap[:, :, i, 0:64], pt.ap[:, 0:512].rearrange("p (h c) -> p h c", h=8), [pt], [Vst2])
                dma("sp", VN_d[:, :, kt0:kt0 + nt, :].rearrange("h p k e -> p h k e"), Vst2.ap[:, :, 0:nt, :], [Vst2], [dt_["VN"]], "st_vst2")
            P.barrier()

            wload(0, rslot(0, 8, 1024), w_in_v[:, :, 0:1024])
            wload(1, rslot(1, 8, 1024), w_in_v[:, :, 1024:2048])
            wmo_v = ring[2].ap[:, 0:4096].rearrange("p (k n) -> p k n", k=4)
            wno_v = ring[2].ap[:, 4096:8192].rearrange("p (k n) -> p k n", k=4)
            wload(2, wmo_v, w_mo_d[l].rearrange("(k p) n -> p k n", p=128))
            wload(2, wno_v, w_no_d[l].rearrange("(k p) n -> p k n", p=128))
            wload(3, rslot(3, 8, 1024), w_out_d[l].rearrange("(k p) n -> p k n", p=128))

            AR.reset()
            Kb = [AR.b("Kb", 96, [T]) for _ in range(2)]
            Qb = [AR.b("Qb", 96, [T]) for _ in range(2)]
            Vb = [AR.b("Vb", 128, [2 * NKT * 65]) for _ in range(2)]
            TTb = [AR.b("TT", 64, [15, 64]) for _ in range(2)]
            Pt = [AR.b("Pt", 128, [512]) for _ in range(4)]
            yst = [AR.b("yst%d" % _i, 64, [512]) for _i in range(2)]
            rhi = AR.b("rhi", 128, [512])
            rlo = AR.b("rlo", 128, [512])
            yraw = AR.f("yraw", 128, [512])
            rec = AR.f("rec", 128, [512])
            ptc = [0]
            pac = [0]

            def load_head(hi):
                i = hi % 2
                if hi < 8:
                    dma("sp", Kb[i].ap[0:96, :], KM_d[hi], [dt_["KM"]], [Kb[i]], "ld_k%d" % i)
                    dma("sp", Qb[i].ap[0:96, :], QM_d[hi], [dt_["QM"]], [Qb[i]], "ld_q%d" % i)
                    dma("sp", Vb[i].ap[:, 0:NKT * 65], VM_d[hi].rearrange("p k e -> p (k e)"), [dt_["VM"]], [Vb[i]], "ld_v%d" % i)
                else:
                    h = hi - 8
                    dma("sp", Kb[i].ap[0:64, :], KN_d[h], [dt_["KN"]], [Kb[i]], "ld_k%d" % i)
                    dma("sp", Qb[i].ap[0:64, :], QN_d[h], [dt_["QN"]], [Qb[i]], "ld_q%d" % i)
                    for s_ in range(2):
                        dma("sp", Vb[i].ap[0:64, :].rearrange("p (k s e) -> p k s e", s=2, e=65)[:, :, s_, :],
                            VN_d[h, 64 * s_:64 * s_ + 64, :, :], [dt_["VN"]], [Vb[i]], "ld_v%d" % i)

            def attn_block(hi, q0, qw, tiles):
                i = hi % 2
                dd = 96 if hi < 8 else 64
                K_, Q_, V_ = Kb[i], Qb[i], Vb[i]
                Vv = V_.ap[:, :].rearrange("p (k e) -> p k e", e=65)
                acc = psb[6 + (pac[0] % 2)]
                pac[0] += 1
                nt_ = len(tiles)
                for ti, (k0, kn, c0, cn, vs, bias) in enumerate(tiles):
                    ps_ = psb[ptc[0] % 5]
                    pt_ = Pt[ptc[0] % 4]
                    ptc[0] += 1
                    mm(ps_.ap[0:kn, 0:cn], K_.ap[0:dd, k0:k0 + kn], Q_.ap[0:dd, q0 + c0:q0 + c0 + cn], True, bias is None,
                       [K_, Q_], [ps_])
                    if bias is not None:
                        mm(ps_.ap[0:kn, 0:cn], ident.ap[0:64, 0:64], bias, False, True, [ident, TTb[i]], [ps_])
                    act(pt_.ap[0:kn, 0:cn], ps_.ap[0:kn, 0:cn], AF.Exp, [ps_], [pt_])
                    mm(acc.ap[0:65, c0:c0 + cn], Vv[0:kn, vs, :], pt_.ap[0:kn, 0:cn], ti == 0, ti == nt_ - 1,
                       [V_, pt_], [acc], skip_group_check=True)
                ys = yst[pac[0] % 2]
                act(yraw.ap[0:65, 0:qw], acc.ap[0:65, 0:qw], AF.Copy, [acc], [yraw])
                recip(rec.ap[64:65, 0:qw], yraw.ap[64:65, 0:qw], [yraw], [rec])
                cp(rhi.ap[64:65, 0:qw], rec.ap[64:65, 0:qw], [rec], [rhi])
                tt(rlo.ap[64:65, 0:qw], rec.ap[64:65, 0:qw], rhi.ap[64:65, 0:qw], ALU.subtract, [rec, rhi], [rlo])
                pbc = psb[5]
                mm(pbc.ap[0:64, 0:qw], ones.ap[64:65, 0:64], rhi.ap[64:65, 0:qw], True, False, [ones, rhi], [pbc])
                mm(pbc.ap[0:64, 0:qw], ones.ap[64:65, 0:64], rlo.ap[64:65, 0:qw], False, True, [ones, rlo], [pbc])
                tt(ys.ap[0:64, 0:qw], yraw.ap[0:64, 0:qw], pbc.ap[0:64, 0:qw], ALU.mult, [yraw, pbc], [ys])
                dma("sp", YT_d[hi, :, q0:q0 + qw], ys.ap[0:64, 0:qw], [ys], [dt_["YT"]], "st_" + ys.name)

            def build_tt(hi):
                i = hi % 2
                h = hi - 8
                TTt = TTb[i]
                for half in range(2):
                    pt = psb[ptc[0] % 5]
                    ptc[0] += 1
                    for qq in range(32):
                        qc = half * 32 + qq
                        mm(pt.ap[0:64, qq * 15:(qq + 1) * 15], dpad.ap[0:31, 63 - qc:127 - qc], rpbT.ap[0:31, h * 15:(h + 1) * 15],
                           True, True, [dpad, rpbT], [pt])
                    tt(TTt.ap[0:64, :, half * 32:(half + 1) * 32],
                       pt.ap[0:64, 0:480].rearrange("p (q a) -> p a q", a=15),
                       cmask.ap[0:64, half * 32:(half + 1) * 32].unsqueeze(1).to_broadcast([64, 15, 32]),
                       ALU.add, [pt, cmask], [TTt])

            load_head(0)
            for hi in range(16):
                if hi + 1 < 16:
                    load_head(hi + 1)
                i = hi % 2
                if hi >= 8:
                    build_tt(hi)
                if hi < 8:
                    ctx_tiles = [(kt * 128, 128, 0, CTX, kt, None) for kt in range(2)]
                else:
                    ctx_tiles = [(kt * 64, 64, 0, CTX, kt, None) for kt in range(4)]
                if not last:
                    attn_block(hi, 0, CTX, ctx_tiles)
                for qb in range(4):
                    q0 = CTX + 512 * qb
                    if hi < 8:
                        tiles = [(kt * 128, 128, 0, 512, kt, None) for kt in range(NKT)]
                    else:
                        tiles = [(kt * 64, 64, 0, 512, kt, None) for kt in range(4)]
                        R = 8 * qb
                        for kr in range(32):
                            vr = na_valid_rows(kr, R)
                            if vr is None:
                                continue
                            qlo, qhi = vr
                            a0 = 7 + qlo - kr
                            nr = qhi - qlo
                            assert 0 <= a0 and a0 + nr <= 15
                            bias = TTb[i].ap[0:64, a0:a0 + nr, :].rearrange("p a q -> p (a q)")
                            tiles.append((CTX + 64 * kr, 64, (qlo - R) * 64, nr * 64, 4 + kr, bias))
                    attn_block(hi, q0, 512, tiles)
            P.barrier()

            AR.reset()
            hT = AR.b("hT", 128, [8, 512])
            ymn = [AR.b("ymn", 128, [8, 512]) for _ in range(2)]
            yc = AR.b("yc", 128, [8, 512])
            rt = AR.f("rt", 128, [512])
            rstd = AR.f("rstd", 128, [512])
            tmpf = [AR.f("tmpf", 128, [512]) for _ in range(2)]
            wg0 = rslot(0, 8, 1024)
            wg1 = rslot(1, 8, 1024)
            wo_v = rslot(3, 8, 1024)
            blocks = token_blocks(not last)

            def load_y(bi):
                t0, w, col = blocks[bi]
                y_ = ymn[bi % 2]
                for typ in range(2):
                    dma("sp", y_.ap[:, 4 * typ:4 * typ + 4, 0:w],
                        YT_d[8 * typ:8 * typ + 8, :, t0:t0 + w].rearrange("(j a) d t -> (a d) j t", a=2),
                        [dt_["YT"]], [y_], "ld_y%d" % (bi % 2))

            load_y(0)
            for bi, (t0, w, col) in enumerate(blocks):
                if bi + 1 < len(blocks):
                    load_y(bi + 1)
                y_ = ymn[bi % 2]
                norm_mod(hT, t0, w, col, A1, 0, rt, rstd, tmpf)
                for j in range(8):
                    pg1, pg2, pm1, pm2 = psum(), psum(), psum(), psum()
                    for kc in range(8):
                        mm(pg1.ap[:, 0:w], wg0[:, kc, j * 128:(j + 1) * 128], hT.ap[:, kc, 0:w], kc == 0, kc == 7, [ring[0], hT], [pg1])
                    for kc in range(8):
                        mm(pg2.ap[:, 0:w], wg1[:, kc, j * 128:(j + 1) * 128], hT.ap[:, kc, 0:w], kc == 0, kc == 7, [ring[1], hT], [pg2])
                    for kc in range(4):
                        mm(pm1.ap[:, 0:w], wmo_v[:, kc, j * 128:(j + 1) * 128], y_.ap[:, kc, 0:w], kc == 0, kc == 3, [ring[2], y_], [pm1])
                    for kc in range(4):
                        mm(pm2.ap[:, 0:w], wno_v[:, kc, j * 128:(j + 1) * 128], y_.ap[:, 4 + kc, 0:w], kc == 0, kc == 3, [ring[2], y_], [pm2])
                    s1, s2 = tmpf
                    act(s1.ap[:, 0:w], pg1.ap[:, 0:w], AF.Sigmoid, [pg1], [s1])
                    act(s2.ap[:, 0:w], pg2.ap[:, 0:w], AF.Sigmoid, [pg2], [s2])
                    tt(s1.ap[:, 0:w], s1.ap[:, 0:w], pm1.ap[:, 0:w], ALU.mult, [s1, pm1], [s1])
                    tt(s2.ap[:, 0:w], s2.ap[:, 0:w], pm2.ap[:, 0:w], ALU.mult, [s2, pm2], [s2])
                    tt(yc.ap[:, j, 0:w], s1.ap[:, 0:w], s2.ap[:, 0:w], ALU.add, [s1, s2], [yc])
                for i_ in range(8):
                    po = psum()
                    for j in range(8):
                        mm(po.ap[:, 0:w], wo_v[:, j, i_ * 128:(i_ + 1) * 128], yc.ap[:, j, 0:w], j == 0, j == 7, [ring[3], yc], [po])
                    stt(xT.ap[:, i_, t0:t0 + w], po.ap[:, 0:w], mod.ap[:, 16 + i_, col:col + 1], xT.ap[:, i_, t0:t0 + w],
                        ALU.mult, ALU.add, [po, mod, xT], [xT])
            P.barrier()
            if debug:
                dma("sp", xdbg_d[0].rearrange("(k p) t -> p k t", p=128), xT.ap[:, :, :], [xT], [dt_["xdbg"]], "dbg0")

            AR.reset()
            h2T = AR.b("h2T", 128, [8, 512])
            hid = AR.b("hid", 128, [32, 512])
            rt = AR.f("rt", 128, [512])
            rstd = AR.f("rstd", 128, [512])
            tmpf = [AR.f("tmpf", 128, [512]) for _ in range(2)]
            w1_v = w_ff1_d[l].rearrange("(k p) n -> p k n", p=128)
            w2_v = w_ff2_d[l].rearrange("(k p) n -> p k n", p=128)
            sl = [0]
            for bi, (t0, w, col) in enumerate(blocks):
                norm_mod(h2T, t0, w, col, A2, 24, rt, rstd, tmpf)
                for g in range(4):
                    s = sl[0] % 4
                    sl[0] += 1
                    wv = rslot(s, 8, 1024)
                    wload(s, wv, w1_v[:, :, g * 1024:(g + 1) * 1024])
                    for hc_ in range(8):
                        pt = psum()
                        for kc in range(8):
                            mm(pt.ap[:, 0:w], wv[:, kc, hc_ * 128:(hc_ + 1) * 128], h2T.ap[:, kc, 0:w], kc == 0, kc == 7, [ring[s], h2T], [pt])
                        tm = tmpf[hc_ % 2]
                        act(tm.ap[:, 0:w], pt.ap[:, 0:w], AF.Relu, [pt], [tm])
                        tt(hid.ap[:, g * 8 + hc_, 0:w], tm.ap[:, 0:w], tm.ap[:, 0:w], ALU.mult, [tm], [hid])
                for oc in range(4):
                    s = sl[0] % 4
                    sl[0] += 1
                    wv = rslot(s, 32, 256)
                    wload(s, wv, w2_v[:, :, oc * 256:(oc + 1) * 256])
                    for i2 in range(2):
                        i_ = oc * 2 + i2
                        po = psum()
                        for hc_ in range(32):
                            mm(po.ap[:, 0:w], wv[:, hc_, i2 * 128:(i2 + 1) * 128], hid.ap[:, hc_, 0:w], hc_ == 0, hc_ == 31, [ring[s], hid], [po])
                        stt(xT.ap[:, i_, t0:t0 + w], po.ap[:, 0:w], mod.ap[:, 40 + i_, col:col + 1], xT.ap[:, i_, t0:t0 + w],
                            ALU.mult, ALU.add, [po, mod, xT], [xT])
            P.barrier()
            if debug:
                dma("sp", xdbg_d[1].rearrange("(k p) t -> p k t", p=128), xT.ap[:, :, :], [xT], [dt_["xdbg"]], "dbg1")

        dma("sp", outT_d.rearrange("(k p) t -> p k t", p=128), xT.ap[:, :, CTX:T], [xT], [dt_["out"]], "outst")
        P.op("sp", None, reads=[dt_["out"], dt_["xdbg"]])
        P.emit()
    return nc


def _consts():
    t = np.arange(SEQ)
    inv = (10000.0 ** (-np.arange(8, dtype=np.float32) / 8.0)).astype(np.float32)
    ang_r = (t // 64).astype(np.float32)[:, None] * inv[None, :]
    ang_c = (t % 64).astype(np.float32)[:, None] * inv[None, :]
    cosT = np.ones((96, T), np.float32)
    sinT = np.zeros((96, T), np.float32)
    for i in range(8):
        for base, ang in ((64, ang_r), (80, ang_c)):
            cosT[base + i, CTX:] = np.cos(ang[:, i])
            cosT[base + 8 + i, CTX:] = np.cos(ang[:, i])
            sinT[base + i, CTX:] = np.sin(ang[:, i])
            sinT[base + 8 + i, CTX:] = np.sin(ang[:, i])
    rotT = np.zeros((96, 96), np.float32)
    for base in (64, 80):
        for i in range(8):
            rotT[base + 8 + i, base + i] = -1.0
            rotT[base + i, base + 8 + i] = 1.0
    dpad = np.zeros((31, 127), np.float32)
    for b in range(31):
        dpad[b, b + 48] = 1.0
    cm = np.full((64, 64), NEG, np.float32)
    for qc in range(64):
        cs = min(max(qc - 8, 0), 48)
        cm[cs:cs + 16, qc] = 0.0
    ident = np.eye(128, dtype=np.float32)
    bdiag = np.zeros((128, 128), np.float32)
    bdiag[0:64, 0:64] = 1.0
    bdiag[64:128, 64:128] = 1.0
    return dict(cosT=cosT, sinT=sinT, rotT=rotT, dpad=dpad, colmask=cm, ident=ident, bdiag=bdiag)


def _fm(v, nch):
    Lh = v.shape[0]
    return np.ascontiguousarray(v.reshape(Lh, nch, 128).transpose(0, 2, 1))


def make_in_maps(inp, depth=DEPTH):
    f = lambda a: np.ascontiguousarray(np.asarray(a, dtype=np.float32))
    L = depth
    shared = dict(
        w_ada=f(inp["w_ada"][:L]), b_adaT=_fm(f(inp["b_ada"][:L]), 48), g_attnT=_fm(f(inp["g_attn"][:L]), 8),
        w_in=f(inp["w_in"][:L]), g_qaT=_fm(f(inp["g_qa"][:L]), 6), w_uq=f(inp["w_uq"][:L]),
        g_kvaT=_fm(f(inp["g_kva"][:L]), 2), w_ukv=f(inp["w_ukv"][:L]),
        g_mq=f(inp["g_mla_q"][:L]).reshape(L, 96, 1), g_mk=f(inp["g_mla_k"][:L]).reshape(L, 96, 1),
        g_nq2=np.ascontiguousarray(np.tile(f(inp["g_na_q"][:L]), (1, 2)).reshape(L, 128, 1)),
        g_nk2=np.ascontiguousarray(np.tile(f(inp["g_na_k"][:L]), (1, 2)).reshape(L, 128, 1)),
        rpbT=np.ascontiguousarray(f(inp["rpb"][:L])[:, :, ::-1, :].transpose(0, 3, 1, 2).reshape(L, 31, 120)),
        w_mla_o=f(inp["w_mla_o"][:L]), w_na_o=f(inp["w_na_o"][:L]), w_out=f(inp["w_out"][:L]),
        g_mlpT=_fm(f(inp["g_mlp"][:L]), 8), w_ff1=f(inp["w_ff1"][:L]), w_ff2=f(inp["w_ff2"][:L]),
    )
    shared.update(_consts())
    x = f(inp["x"])
    ctx = f(inp["ctx"])
    c = f(inp["c"])
    c_ctx = f(inp["c_ctx"])
    maps = []
    for b in range(x.shape[0]):
        m = dict(shared)
        m["xT"] = np.ascontiguousarray(np.concatenate([ctx[b], x[b]], axis=0).T)
        cT = np.stack([c[b], c_ctx], axis=1).reshape(8, 128, 2).transpose(1, 0, 2).reshape(128, 16)
        m["cT"] = np.ascontiguousarray(cT)
        maps.append(m)
    return maps


_NC_CACHE = {}


def kernel(**inputs):
    if "full" not in _NC_CACHE:
        _NC_CACHE["full"] = build_program(DEPTH, False)
    nc = _NC_CACHE["full"]
    maps = make_in_maps(inputs, DEPTH)
    res = run_bass_kernel_spmd(nc, maps, core_ids=list(range(8)))
    out = np.stack([np.asarray(r["outT"]).T for r in res.results], axis=0)
    return np.ascontiguousarray(out.astype(np.float32))
```

```python
import math
from contextlib import ExitStack

import numpy as np
import concourse.bass as bass
import concourse.mybir as mybir
from concourse.bass_utils import run_bass_kernel_spmd

F32 = mybir.dt.float32
BF16 = mybir.dt.bfloat16
ALU = mybir.AluOpType
AF = mybir.ActivationFunctionType

ENGS = ("pe", "act", "dve", "pool", "sp")
BAR_ENGS = ("pe", "act", "dve", "pool", "sp")

D = 1024
SEQ = 2048
CTX = 256
T = CTX + SEQ
NKT = T // 128
DEPTH = 4
IN_COLS = 4640
EPS = 1e-6
NEG = -30000.0


class Tile:
    __slots__ = ("name", "w", "rs", "ap")

    def __init__(self, name, ap=None):
        self.name = name
        self.w = None
        self.rs = {}
        self.ap = ap

    def __getitem__(self, k):
        return self.ap[k]


class Op:
    __slots__ = ("eng", "fn", "deps", "needs_inc", "seq", "dma", "dtok", "bar")

    def __init__(self, eng, fn):
        self.eng = eng
        self.fn = fn
        self.deps = []
        self.needs_inc = False
        self.seq = 0
        self.dma = False
        self.dtok = None
        self.bar = True


class Prog:
    def __init__(self, nc):
        self.nc = nc
        self.ops = {e: [] for e in ENGS}
        self.dma_cnt = {}
        self.dma_last = {}
        self.last = {}

    def _add_dep(self, o, d):
        if d is None or d is o:
            return
        if (not d.dma) and (not o.dma) and d.eng == "pe" and o.eng == "pe":
            return
        if d not in o.deps:
            o.deps.append(d)

    def op(self, eng, fn, reads=(), writes=(), dma_key=None, bar=True):
        o = Op(eng, fn)
        if dma_key is not None:
            o.dma = True
            o.bar = bar
            n = self.dma_cnt.get(dma_key, 0) + 1
            self.dma_cnt[dma_key] = n
            o.dtok = (dma_key, 16 * n)
            if bar:
                self.dma_last[dma_key] = o
        else:
            self.last[eng] = o
        for t in reads:
            self._add_dep(o, t.w)
        for t in writes:
            self._add_dep(o, t.w)
            for r in t.rs.values():
                self._add_dep(o, r)
        for d in o.deps:
            if not d.dma:
                d.needs_inc = True
        rk = ("dma", dma_key) if o.dma else eng
        for t in reads:
            t.rs[rk] = o
        for t in writes:
            t.w = o
            t.rs = {}
        self.ops[eng].append(o)
        return o

    def barrier(self):
        lasts = [self.last[e] for e in BAR_ENGS if e in self.last]
        dm = list(self.dma_last.values())
        self.dma_last = {}
        for e in BAR_ENGS:
            o = Op(e, None)
            for d in lasts:
                if d.eng != e:
                    o.deps.append(d)
                    d.needs_inc = True
            for d in dm:
                o.deps.append(d)
            self.ops[e].append(o)

    def emit(self):
        nc = self.nc
        for e in ENGS:
            c = 0
            for o in self.ops[e]:
                if o.needs_inc:
                    c += 1
                    o.seq = c
        with ExitStack() as es:
            esem = {e: es.enter_context(nc.semaphore("s_" + e)) for e in ENGS}
            dsem = {}
            for i, k in enumerate(self.dma_cnt):
                dsem[k] = es.enter_context(nc.semaphore("d%d" % i))
            block = es.enter_context(nc.Block())

            def run(e, engobj):
                known = {}
                for o in self.ops[e]:
                    for d in o.deps:
                        if d.dma:
                            sk, val = d.dtok
                            sem = dsem[sk]
                            kk = ("d", sk)
                        else:
                            sem = esem[d.eng]
                            val = d.seq
                            kk = d.eng
                        if known.get(kk, 0) >= val:
                            continue
                        known[kk] = val
                        engobj.wait_ge(sem, val)
                    if o.fn is None:
                        continue
                    ins = o.fn(engobj)
                    if o.dma:
                        ins.then_inc(dsem[o.dtok[0]], 16)
                    elif o.needs_inc:
                        ins.then_inc(esem[e], 1)

            @block.tensor
            def _(eng):
                run("pe", eng)

            @block.scalar
            def _(eng):
                run("act", eng)

            @block.vector
            def _(eng):
                run("dve", eng)

            @block.gpsimd
            def _(eng):
                run("pool", eng)

            @block.sync
            def _(eng):
                run("sp", eng)


def token_blocks(include_ctx=True):
    bl = []
    if include_ctx:
        bl.append((0, CTX, 1))
    for i in range(4):
        bl.append((CTX + 512 * i, 512, 0))
    return bl


def na_valid_rows(kr, R):
    rows = [qr for qr in range(R, R + 8) if min(max(qr - 4, 0), 24) <= kr < min(max(qr - 4, 0), 24) + 8]
    if not rows:
        return None
    assert rows == list(range(rows[0], rows[-1] + 1))
    return rows[0], rows[-1] + 1


def build_program(depth=DEPTH, debug=False):
    nc = bass.Bass("TRN2", target_bir_lowering=False)
    L = depth

    def din(name, shape, dt=F32):
        return nc.dram_tensor(name, list(shape), dt, kind="ExternalInput").ap()

    skind = "ExternalOutput" if debug else "Internal"

    def dscr(name, shape, dt=BF16):
        return nc.dram_tensor(name, list(shape), dt, kind=skind).ap()

    xT_d = din("xT", [D, T])
    cT_d = din("cT", [128, 16])
    w_ada_d = din("w_ada", [L, D, 6 * D])
    b_ada_d = din("b_adaT", [L, 128, 48])
    g_attn_d = din("g_attnT", [L, 128, 8])
    w_in_d = din("w_in", [L, D, IN_COLS])
    g_qa_d = din("g_qaT", [L, 128, 6])
    w_uq_d = din("w_uq", [L, 768, 768])
    g_kva_d = din("g_kvaT", [L, 128, 2])
    w_ukv_d = din("w_ukv", [L, 256, 1024])
    g_mq_d = din("g_mq", [L, 96, 1])
    g_mk_d = din("g_mk", [L, 96, 1])
    g_nq_d = din("g_nq2", [L, 128, 1])
    g_nk_d = din("g_nk2", [L, 128, 1])
    rpbT_d = din("rpbT", [L, 31, 120])
    w_mo_d = din("w_mla_o", [L, 512, D])
    w_no_d = din("w_na_o", [L, 512, D])
    w_out_d = din("w_out", [L, D, D])
    g_mlp_d = din("g_mlpT", [L, 128, 8])
    w_ff1_d = din("w_ff1", [L, D, 4 * D])
    w_ff2_d = din("w_ff2", [L, 4 * D, D])
    cos_d = din("cosT", [96, T])
    sin_d = din("sinT", [96, T])
    rt_d = din("rotT", [96, 96])
    dpad_d = din("dpad", [31, 127])
    cmask_d = din("colmask", [64, 64])
    ident_d = din("ident", [128, 128])
    bdiag_d = din("bdiag", [128, 128])
    outT_d = nc.dram_tensor("outT", [D, SEQ], F32, kind="ExternalOutput").ap()

    QM_d = dscr("QM", [8, 96, T])
    KM_d = dscr("KM", [8, 96, T])
    VM_d = dscr("VM", [8, 128, NKT, 65])
    QN_d = dscr("QN", [8, 64, T])
    KN_d = dscr("KN", [8, 64, T])
    VN_d = dscr("VN", [8, 128, NKT, 65])
    YT_d = dscr("YT", [16, 64, T])
    xdbg_d = dscr("xdbg", [2, D, T], F32) if debug else None

    es = ExitStack()
    with es:
        def sb(name, shape, dt):
            return es.enter_context(nc.sbuf_tensor(name, list(shape), dt))

        P = Prog(nc)

        def ptile(name, shape, dt):
            return Tile(name, sb(name, shape, dt))

        xT = ptile("xTs", [128, 8, T], F32)
        ring = [ptile("ring%d" % i, [128, 8192], BF16) for i in range(4)]
        cosT = ptile("cosTs", [96, T], BF16)
        sinT = ptile("sinTs", [96, T], BF16)
        ident = ptile("idents", [128, 128], BF16)
        ones = ptile("oness", [128, 128], BF16)
        bdiag = ptile("bdiags", [128, 128], BF16)
        rotT = ptile("rotTs", [96, 96], BF16)
        dpad = ptile("dpads", [31, 127], BF16)
        cmask = ptile("cmasks", [64, 64], F32)
        cTs = ptile("cTs", [128, 16], F32)
        siluT = ptile("siluTs", [128, 8, 2], BF16)
        mod = ptile("mods", [128, 48, 2], F32)
        A1 = ptile("A1s", [128, 8, 2], F32)
        A2 = ptile("A2s", [128, 8, 2], F32)
        b_ada = ptile("b_adas", [128, 48], F32)
        g_attn = ptile("g_attns", [128, 8], F32)
        g_mlp = ptile("g_mlps", [128, 8], F32)
        g_qa = ptile("g_qas", [128, 6], F32)
        g_kva = ptile("g_kvas", [128, 2], F32)
        g_mq = ptile("g_mqs", [96, 1], F32)
        g_mk = ptile("g_mks", [96, 1], F32)
        g_nq = ptile("g_nqs", [128, 1], F32)
        g_nk = ptile("g_nks", [128, 1], F32)
        wkr = ptile("wkrs", [128, 8, 96], BF16)
        wkpad = ptile("wkpads", [128, 2, 8, 96], BF16)
        rpbT = ptile("rpbTs", [31, 120], BF16)
        ARB = 21000
        ARF = 2048
        arena_b = sb("arena_b", [128, ARB], BF16)
        arena_f = sb("arena_f", [128, ARF], F32)
        psb = [Tile("ps%d" % i, es.enter_context(nc.psum_tensor("ps%d" % i, [128, 512], F32))) for i in range(8)]

        dt_ = {n: Tile(n) for n in ("QM", "KM", "VM", "QN", "KN", "VN", "YT", "out", "xdbg")}

        class Arena:
            def __init__(self):
                self.ob = 0
                self.of = 0
                self.n = 0

            def reset(self):
                self.ob = 0
                self.of = 0

            def b(self, name, parts, shape):
                n = int(np.prod(shape))
                ap = arena_b[0:parts, self.ob:self.ob + n]
                self.ob += n + (n % 2)
                assert self.ob <= ARB, (name, self.ob)
                return Tile(name, shp(ap, shape))

            def f(self, name, parts, shape):
                n = int(np.prod(shape))
                ap = arena_f[0:parts, self.of:self.of + n]
                self.of += n
                assert self.of <= ARF, (name, self.of)
                return Tile(name, shp(ap, shape))

        def shp(ap, shape):
            if len(shape) == 1:
                return ap
            if len(shape) == 2:
                return ap.rearrange("p (a b) -> p a b", a=shape[0])
            if len(shape) == 3:
                return ap.rearrange("p (a b c) -> p a b c", a=shape[0], b=shape[1])
            raise ValueError

        AR = Arena()

        psc = [0]

        def psum(k=None):
            i = psc[0] % 8 if k is None else k
            if k is None:
                psc[0] += 1
            return psb[i]

        def mm(out, lhsT, rhs, start, stop, reads, writes, **kw):
            return P.op("pe", lambda e: e.matmul(out, lhsT, rhs, start=start, stop=stop, **kw), reads, writes)

        def act(out, in_, func, reads, writes, **kw):
            return P.op("act", lambda e: e.activation(out, in_, func, **kw), reads, writes)

        def tt(out, a, b, op, reads, writes, eng="dve"):
            return P.op(eng, lambda e: e.tensor_tensor(out, a, b, op), reads, writes)

        def stt(out, in0, scalar, in1, op0, op1, reads, writes, eng="dve"):
            return P.op(eng, lambda e: e.scalar_tensor_tensor(out, in0, scalar, in1, op0, op1), reads, writes)

        def ts(out, in0, s1, s2, op0, op1, reads, writes, eng="dve"):
            if s2 is None:
                return P.op(eng, lambda e: e.tensor_scalar(out, in0, s1, None, op0), reads, writes)
            return P.op(eng, lambda e: e.tensor_scalar(out, in0, s1, s2, op0, op1), reads, writes)

        def cp(out, in_, reads, writes, eng="dve"):
            return P.op(eng, lambda e: e.tensor_copy(out, in_), reads, writes)

        def recip(out, in_, reads, writes):
            return P.op("dve", lambda e: e.reciprocal(out, in_), reads, writes)

        def memset(ap, v, writes, eng="dve"):
            return P.op(eng, lambda e: e.memset(ap, v), (), writes)

        def dma(eng, out, in_, reads, writes, key, bar=True):
            return P.op(eng, lambda e: e.dma_start(out=out, in_=in_), reads, writes, dma_key=key, bar=bar)

        def wload(slot, out_ap, in_ap):
            return dma("pool", out_ap, in_ap, (), [ring[slot]], "ring%d" % slot, bar=False)

        def sload(tile, out_ap, in_ap, eng="sp"):
            return dma(eng, out_ap, in_ap, (), [tile], "t_" + tile.name, bar=False)

        def rslot(slot, k, n):
            return ring[slot].ap[:, 0:k * n].rearrange("p (k n) -> p k n", k=k)

        evq = [0]

        def evac(out, in_, reads, writes):
            evq[0] += 1
            if evq[0] % 2:
                return act(out, in_, AF.Copy, reads, writes)
            return cp(out, in_, reads, writes)

        dma("sp", xT.ap[:, :, :], xT_d.rearrange("(k p) t -> p k t", p=128), (), [xT], "xin", bar=False)
        sload(cTs, cTs.ap[:, :], cT_d[:, :])
        sload(cmask, cmask.ap[:, :], cmask_d[:, :])
        for tl, d_ in ((cosT, cos_d), (sinT, sin_d), (ident, ident_d), (bdiag, bdiag_d), (rotT, rt_d), (dpad, dpad_d)):
            sload(tl, tl.ap[:, :], d_[:, :], eng="pool")
        memset(ones.ap[:, :], 1.0, [ones])
        memset(wkr.ap[:, :, :], 0.0, [wkr])
        memset(wkpad.ap[:, :, :, :], 0.0, [wkpad])
        act(siluT.ap[:, :, :].rearrange("p k c -> p (k c)"), cTs.ap[:, :], AF.Silu, [cTs], [siluT])

        def rms_rstd(sq_chunks, sq_tiles, nfeat, parts, w, rt, rstd, lhs_ones, lhs_tile):
            pt = psum()
            n = len(sq_chunks)
            for i, s in enumerate(sq_chunks):
                mm(pt.ap[0:parts, 0:w], lhs_ones, s, i == 0, i == n - 1, [lhs_tile] + sq_tiles, [pt])
            act(rstd.ap[0:parts, 0:w], pt.ap[0:parts, 0:w], AF.Ln, [pt], [rstd], scale=1.0 / nfeat, bias=EPS)
            act(rstd.ap[0:parts, 0:w], rstd.ap[0:parts, 0:w], AF.Exp, [rstd], [rstd], scale=-0.5)

        def norm_mod(hT, t0, w, col, Aa, sh_idx, rt, rstd, tmp):
            act(hT.ap[:, :, 0:w], xT.ap[:, :, t0:t0 + w], AF.Square, [xT], [hT])
            rms_rstd([hT.ap[:, kc, 0:w] for kc in range(8)], [hT], float(D), 128, w, rt, rstd, ones.ap[:, :], ones)
            for kc in range(8):
                tm = tmp[kc % len(tmp)]
                stt(tm.ap[:, 0:w], xT.ap[:, kc, t0:t0 + w], Aa.ap[:, kc, col:col + 1], rstd.ap[:, 0:w],
                    ALU.mult, ALU.mult, [xT, Aa, rstd], [tm])
                act(hT.ap[:, kc, 0:w], tm.ap[:, 0:w], AF.Identity, [tm, mod], [hT],
                    bias=mod.ap[:, sh_idx + kc, col:col + 1], scale=1.0)

        for l in range(L):
            last = (l == L - 1)
            sload(b_ada, b_ada.ap[:, :], b_ada_d[l])
            sload(g_attn, g_attn.ap[:, :], g_attn_d[l])
            sload(g_mlp, g_mlp.ap[:, :], g_mlp_d[l])
            sload(g_qa, g_qa.ap[:, :], g_qa_d[l])
            sload(g_kva, g_kva.ap[:, :], g_kva_d[l])
            sload(g_mq, g_mq.ap[:, :], g_mq_d[l])
            sload(g_mk, g_mk.ap[:, :], g_mk_d[l])
            sload(g_nq, g_nq.ap[:, :], g_nq_d[l])
            sload(g_nk, g_nk.ap[:, :], g_nk_d[l])
            sload(rpbT, rpbT.ap[:, :], rpbT_d[l], eng="pool")
            pmod = psum()
            w_ada_v = w_ada_d[l].rearrange("(k p) n -> p k n", p=128)
            for blk in range(6):
                s = blk % 4
                wload(s, rslot(s, 8, 1024), w_ada_v[:, :, blk * 1024:(blk + 1) * 1024])
                wv = rslot(s, 8, 1024)
                for j in range(8):
                    ch = blk * 8 + j
                    for kc in range(8):
                        mm(pmod.ap[:, ch * 2:ch * 2 + 2], wv[:, kc, j * 128:(j + 1) * 128], siluT.ap[:, kc, :],
                           kc == 0, kc == 7, [ring[s], siluT], [pmod])
            tt(mod.ap[:, :, :], pmod.ap[:, 0:96].rearrange("p (a b) -> p a b", b=2),
               b_ada.ap[:, :].unsqueeze(2).to_broadcast([128, 48, 2]), ALU.add, [pmod, b_ada], [mod])
            stt(A1.ap[:, :, :], mod.ap[:, 8:16, :], 1.0, g_attn.ap[:, :].unsqueeze(2).to_broadcast([128, 8, 2]),
                ALU.add, ALU.mult, [mod, g_attn], [A1])
            stt(A2.ap[:, :, :], mod.ap[:, 32:40, :], 1.0, g_mlp.ap[:, :].unsqueeze(2).to_broadcast([128, 8, 2]),
                ALU.add, ALU.mult, [mod, g_mlp], [A2])
            ts(g_mq.ap[:, :], g_mq.ap[:, :], 96.0 ** -0.5, None, ALU.mult, ALU.bypass, [g_mq], [g_mq])
            ts(g_nq.ap[:, :], g_nq.ap[:, :], 0.125, None, ALU.mult, ALU.bypass, [g_nq], [g_nq])

            w_in_v = w_in_d[l].rearrange("(k p) n -> p k n", p=128)
            wload(0, rslot(0, 8, 1024), w_in_v[:, :, 2048:3072])
            wload(1, rslot(1, 8, 1024), w_in_v[:, :, 3104:4128])
            wload(2, rslot(2, 8, 512), w_in_v[:, :, 4128:4640])
            uqv = ring[3].ap[:, 0:6 * 768].rearrange("p (k n) -> p k n", k=6)
            ukvv = ring[3].ap[:, 6 * 768:6 * 768 + 2048].rearrange("p (k n) -> p k n", k=2)
            wload(3, uqv, w_uq_d[l].rearrange("(k p) n -> p k n", p=128))
            wload(3, ukvv, w_ukv_d[l].rearrange("(k p) n -> p k n", p=128))
            dma("pool", wkr.ap[:, :, 64:96], w_in_v[:, :, 3072:3104], (), [wkr], "t_wkr", bar=False)
            for kc in range(6):
                ts(uqv[:, kc, :], uqv[:, kc, :], g_qa.ap[:, kc:kc + 1], None, ALU.mult, ALU.bypass, [ring[3], g_qa], [ring[3]])
            for kc in range(2):
                ts(ukvv[:, kc, :], ukvv[:, kc, :], g_kva.ap[:, kc:kc + 1], None, ALU.mult, ALU.bypass, [ring[3], g_kva], [ring[3]])
            for kc in range(2):
                cp(wkpad.ap[:, kc, :, 0:64], ukvv[:, kc, :].rearrange("p (h c) -> p h c", h=8)[:, :, 0:64], [ring[3]], [wkpad])
            w0 = rslot(0, 8, 1024)
            w1 = rslot(1, 8, 1024)
            w2 = rslot(2, 8, 512)

            AR.reset()
            hT = AR.b("hT", 128, [8, 512])
            cq = AR.b("cq", 128, [6, 512])
            sq6 = AR.b("sq6", 128, [6, 512])
            ckv = AR.b("ckv", 128, [2, 512])
            krp = AR.b("krp", 96, [512])
            ugs = [AR.b("ug%d" % _i, 128, [512]) for _i in range(2)]
            usqs = [AR.b("usq%d" % _i, 128, [512]) for _i in range(2)]
            qst = [AR.b("qst%d" % _i, 128, [512]) for _i in range(3)]
            Vst = AR.b("Vst", 128, [8, 4, 65])
            Vst2 = AR.b("Vst2", 128, [8, 4, 65])
            rstds = [AR.f("rstd%d" % _i, 128, [512]) for _i in range(2)]
            rstd = rstds[0]
            rt = None
            tmpf = [AR.f("tmpf", 128, [512]) for _ in range(2)]
            memset(Vst.ap[:, :, :, 64:65], 1.0, [Vst])
            memset(Vst2.ap[:, :, :, 64:65], 1.0, [Vst2])
            hc = [0]

            def head_s1(pu, dd, w, gain):
                i = hc[0] % 2
                hc[0] += 1
                ug, usq = ugs[i], usqs[i]
                act(ug.ap[0:dd, 0:w], pu.ap[0:dd, 0:w], AF.Identity, [pu, gain], [ug], scale=gain.ap[0:dd, 0:1], bias=0.0)
                act(usq.ap[0:dd, 0:w], pu.ap[0:dd, 0:w], AF.Square, [pu], [usq])
                return i

            def head_s2(i, dd, w, t0, rope, bd, dst, dst_tile):
                ug, usq, q_, rs_ = ugs[i], usqs[i], qst[hc[0] % 3], rstds[i]
                if bd:
                    rms_rstd([usq.ap[0:dd, 0:w]], [usq], 64.0, dd, w, None, rs_, bdiag.ap[0:dd, 0:dd], bdiag)
                else:
                    rms_rstd([usq.ap[0:dd, 0:w]], [usq], float(dd), dd, w, None, rs_, ones.ap[0:dd, 0:dd], ones)
                if rope:
                    pr = psum()
                    mm(pr.ap[0:dd, 0:w], rotT.ap[0:dd, 0:dd], ug.ap[0:dd, 0:w], True, True, [rotT, ug], [pr])
                    t1, t2 = tmpf
                    tt(t1.ap[0:dd, 0:w], ug.ap[0:dd, 0:w], cosT.ap[0:dd, t0:t0 + w], ALU.mult, [ug, cosT], [t1])
                    tt(t2.ap[0:dd, 0:w], pr.ap[0:dd, 0:w], sinT.ap[0:dd, t0:t0 + w], ALU.mult, [pr, sinT], [t2])
                    tt(t1.ap[0:dd, 0:w], t1.ap[0:dd, 0:w], t2.ap[0:dd, 0:w], ALU.add, [t1, t2], [t1], eng="pool")
                    tt(q_.ap[0:dd, 0:w], t1.ap[0:dd, 0:w], rs_.ap[0:dd, 0:w], ALU.mult, [t1, rs_], [q_], eng="pool")
                else:
                    tt(q_.ap[0:dd, 0:w], ug.ap[0:dd, 0:w], rs_.ap[0:dd, 0:w], ALU.mult, [ug, rs_], [q_])
                dma("sp", dst, q_.ap[0:dd, 0:w], [q_], [dst_tile], "st_" + q_.name)

            def run_jobs(jobs):
                prev = None
                for j1, j2 in jobs:
                    i = j1()
                    if prev is not None:
                        prev[1](prev[0])
                    prev = (i, j2)
                if prev is not None:
                    prev[1](prev[0])

            for (t0, w, col) in token_blocks(True):
                nt = w // 128
                kt0 = t0 // 128
                norm_mod(hT, t0, w, col, A1, 0, rt, rstd, tmpf)
                for j in range(8):
                    pt = psum()
                    for kc in range(8):
                        mm(pt.ap[:, 0:w], w0[:, kc, j * 128:(j + 1) * 128], hT.ap[:, kc, 0:w], kc == 0, kc == 7, [ring[0], hT], [pt])
                    if j < 6:
                        evac(cq.ap[:, j, 0:w], pt.ap[:, 0:w], [pt], [cq])
                    else:
                        evac(ckv.ap[:, j - 6, 0:w], pt.ap[:, 0:w], [pt], [ckv])
                act(sq6.ap[:, :, 0:w], cq.ap[:, :, 0:w], AF.Square, [cq], [sq6])
                rms_rstd([sq6.ap[:, j, 0:w] for j in range(6)], [sq6], 768.0, 128, w, rt, rstd, ones.ap[:, :], ones)
                tt(cq.ap[:, :, 0:w], cq.ap[:, :, 0:w], rstd.ap[:, 0:w].unsqueeze(1).to_broadcast([128, 6, w]), ALU.mult, [cq, rstd], [cq])
                act(sq6.ap[:, 0:2, 0:w], ckv.ap[:, :, 0:w], AF.Square, [ckv], [sq6])
                rms_rstd([sq6.ap[:, j, 0:w] for j in range(2)], [sq6], 256.0, 128, w, rt, rstd, ones.ap[:, :], ones)
                tt(ckv.ap[:, :, 0:w], ckv.ap[:, :, 0:w], rstd.ap[:, 0:w].unsqueeze(1).to_broadcast([128, 2, w]), ALU.mult, [ckv, rstd], [ckv])
                pt = psum()
                for kc in range(8):
                    mm(pt.ap[0:96, 0:w], wkr.ap[:, kc, :], hT.ap[:, kc, 0:w], kc == 0, kc == 7, [wkr, hT], [pt])
                evac(krp.ap[0:96, 0:w], pt.ap[0:96, 0:w], [pt], [krp])
                jobs = []
                for h in range(8):
                    def j1(h=h):
                        pt = psum()
                        for kc in range(6):
                            mm(pt.ap[0:96, 0:w], uqv[:, kc, h * 96:(h + 1) * 96], cq.ap[:, kc, 0:w], kc == 0, kc == 5, [ring[3], cq], [pt])
                        return head_s1(pt, 96, w, g_mq)
                    def j2(i, h=h):
                        head_s2(i, 96, w, t0, True, False, QM_d[h, :, t0:t0 + w], dt_["QM"])
                    jobs.append((j1, j2))
                for h in range(8):
                    def j1(h=h):
                        pt = psum()
                        for kc in range(2):
                            mm(pt.ap[0:96, 0:w], wkpad.ap[:, kc, h, :], ckv.ap[:, kc, 0:w], kc == 0, False, [wkpad, ckv], [pt])
                        mm(pt.ap[0:96, 0:w], ident.ap[0:96, 0:96], krp.ap[0:96, 0:w], False, True, [ident, krp], [pt])
                        return head_s1(pt, 96, w, g_mk)
                    def j2(i, h=h):
                        head_s2(i, 96, w, t0, True, False, KM_d[h, :, t0:t0 + w], dt_["KM"])
                    jobs.append((j1, j2))
                for qk in range(2):
                    for jp in range(4):
                        def j1(qk=qk, jp=jp):
                            pt = psum()
                            c0 = qk * 512 + jp * 128
                            for kc in range(8):
                                mm(pt.ap[:, 0:w], w1[:, kc, c0:c0 + 128], hT.ap[:, kc, 0:w], kc == 0, kc == 7, [ring[1], hT], [pt])
                            return head_s1(pt, 128, w, g_nq if qk == 0 else g_nk)
                        def j2(i, qk=qk, jp=jp):
                            dd_ = (QN_d if qk == 0 else KN_d)[2 * jp:2 * jp + 2, :, t0:t0 + w].rearrange("a d t -> (a d) t")
                            head_s2(i, 128, w, t0, False, True, dd_, dt_["QN" if qk == 0 else "KN"])
                        jobs.append((j1, j2))
                run_jobs(jobs)
                for i in range(nt):
                    pt = psum()
                    for kc in range(2):
                        mm(pt.ap[:, 0:512].rearrange("p (h c) -> p h c", h=8), ckv.ap[:, kc, i * 128:(i + 1) * 128],
                           ukvv[:, kc, :].rearrange("p (h c) -> p h c", h=8)[:, :, 64:128], kc == 0, kc == 1, [ckv, ring[3]], [pt])
                    evac(Vst.ap[:, :, i, 0:64], pt.ap[:, 0:512].rearrange("p (h c) -> p h c", h=8), [pt], [Vst])
                dma("sp", VM_d[:, :, kt0:kt0 + nt, :].rearrange("h p k e -> p h k e"), Vst.ap[:, :, 0:nt, :], [Vst], [dt_["VM"]], "st_vst")
                for i in range(nt):
                    pt = psum()
                    for kc in range(8):
                        mm(pt.ap[:, 0:512], hT.ap[:, kc, i * 128:(i + 1) * 128], w2[:, kc, :], kc == 0, kc == 7, [hT, ring[2]], [pt])
                    evac(Vst2.ap[:, :, i, 0:64], pt.ap[:, 0:512].rearrange("p (h c) -> p h c", h=8), [pt], [Vst2])
                dma("sp", VN_d[:, :, kt0:kt0 + nt, :].rearrange("h p k e -> p h k e"), Vst2.ap[:, :, 0:nt, :], [Vst2], [dt_["VN"]], "st_vst2")
            P.barrier()

            wload(0, rslot(0, 8, 1024), w_in_v[:, :, 0:1024])
            wload(1, rslot(1, 8, 1024), w_in_v[:, :, 1024:2048])
            wmo_v = ring[2].ap[:, 0:4096].rearrange("p (k n) -> p k n", k=4)
            wno_v = ring[2].ap[:, 4096:8192].rearrange("p (k n) -> p k n", k=4)
            wload(2, wmo_v, w_mo_d[l].rearrange("(k p) n -> p k n", p=128))
            wload(2, wno_v, w_no_d[l].rearrange("(k p) n -> p k n", p=128))
            wload(3, rslot(3, 8, 1024), w_out_d[l].rearrange("(k p) n -> p k n", p=128))

            AR.reset()
            Kb = [AR.b("Kb", 96, [T]) for _ in range(2)]
            Qb = [AR.b("Qb", 96, [T]) for _ in range(2)]
            Vb = [AR.b("Vb", 128, [2 * NKT * 65]) for _ in range(2)]
            TTb = [AR.b("TT", 64, [15, 64]) for _ in range(2)]
            Pt = [AR.b("Pt", 128, [512]) for _ in range(4)]
            yst = [AR.b("yst%d" % _i, 64, [512]) for _i in range(2)]
            rhi = [AR.b("rhi%d" % _i, 128, [512]) for _i in range(2)]
            rlo = [AR.b("rlo%d" % _i, 128, [512]) for _i in range(2)]
            yraw = [AR.f("yraw%d" % _i, 128, [512]) for _i in range(2)]
            rec = [AR.f("rec%d" % _i, 128, [512]) for _i in range(2)]
            ptc = [0]
            pqc = [0]
            LA = 2
            DEF = 6

            def load_head(hi):
                i = hi % 2
                if hi < 8:
                    dma("sp", Kb[i].ap[0:96, :], KM_d[hi], [dt_["KM"]], [Kb[i]], "ld_k%d" % i)
                    dma("sp", Qb[i].ap[0:96, :], QM_d[hi], [dt_["QM"]], [Qb[i]], "ld_q%d" % i)
                    dma("sp", Vb[i].ap[:, 0:NKT * 65], VM_d[hi].rearrange("p k e -> p (k e)"), [dt_["VM"]], [Vb[i]], "ld_v%d" % i)
                else:
                    h = hi - 8
                    dma("sp", Kb[i].ap[0:64, :], KN_d[h], [dt_["KN"]], [Kb[i]], "ld_k%d" % i)
                    dma("sp", Qb[i].ap[0:64, :], QN_d[h], [dt_["QN"]], [Qb[i]], "ld_q%d" % i)
                    for s_ in range(2):
                        dma("sp", Vb[i].ap[0:64, :].rearrange("p (k s e) -> p k s e", s=2, e=65)[:, :, s_, :],
                            VN_d[h, 64 * s_:64 * s_ + 64, :, :], [dt_["VN"]], [Vb[i]], "ld_v%d" % i)

            def build_tt(hi):
                i = hi % 2
                h = hi - 8
                TTt = TTb[i]
                for half in range(2):
                    pt = psb[ptc[0] % 5]
                    ptc[0] += 1
                    for qq in range(32):
                        qc = half * 32 + qq
                        mm(pt.ap[0:64, qq * 15:(qq + 1) * 15], dpad.ap[0:31, 63 - qc:127 - qc], rpbT.ap[0:31, h * 15:(h + 1) * 15],
                           True, True, [dpad, rpbT], [pt])
                    tt(TTt.ap[0:64, :, half * 32:(half + 1) * 32],
                       pt.ap[0:64, 0:480].rearrange("p (q a) -> p a q", a=15),
                       cmask.ap[0:64, half * 32:(half + 1) * 32].unsqueeze(1).to_broadcast([64, 15, 32]),
                       ALU.add, [pt, cmask], [TTt])

            def do_qk(e):
                hi, i = e["hi"], e["hi"] % 2
                dd = 96 if hi < 8 else 64
                k0, kn, c0, cn, vs, bias = e["tile"]
                q0 = e["q0"]
                ps_ = psb[ptc[0] % 5]
                pt_ = Pt[pqc[0] % 4]
                ptc[0] += 1
                pqc[0] += 1
                e["pt"] = pt_
                mm(ps_.ap[0:kn, 0:cn], Kb[i].ap[0:dd, k0:k0 + kn], Qb[i].ap[0:dd, q0 + c0:q0 + c0 + cn], True, bias is None,
                   [Kb[i], Qb[i]], [ps_])
                if bias is not None:
                    mm(ps_.ap[0:kn, 0:cn], ident.ap[0:64, 0:64], bias, False, True, [ident, TTb[i]], [ps_])
                act(pt_.ap[0:kn, 0:cn], ps_.ap[0:kn, 0:cn], AF.Exp, [ps_], [pt_])

            f2p = []

            def flush_f2(force_set=None, all_=False):
                while f2p and (all_ or f2p[0][0] <= 0 or (force_set is not None and any(x[1] == force_set for x in f2p))):
                    f2p.pop(0)[2]()

            def do_pv(e):
                hi, i = e["hi"], e["hi"] % 2
                k0, kn, c0, cn, vs, bias = e["tile"]
                acc = e["acc"]
                pt_ = e["pt"]
                Vv = Vb[i].ap[:, :].rearrange("p (k e) -> p k e", e=65)
                mm(acc.ap[0:65, c0:c0 + cn], Vv[0:kn, vs, :], pt_.ap[0:kn, 0:cn], e["first"], e["last"],
                   [Vb[i], pt_], [acc], skip_group_check=True)
                if not e["last"]:
                    return
                s_ = e["set"]
                q0, qw = e["q0"], e["qw"]
                flush_f2(force_set=s_)
                act(yraw[s_].ap[0:65, 0:qw], acc.ap[0:65, 0:qw], AF.Copy, [acc], [yraw[s_]])
                recip(rec[s_].ap[64:65, 0:qw], yraw[s_].ap[64:65, 0:qw], [yraw[s_]], [rec[s_]])
                cp(rhi[s_].ap[64:65, 0:qw], rec[s_].ap[64:65, 0:qw], [rec[s_]], [rhi[s_]])
                tt(rlo[s_].ap[64:65, 0:qw], rec[s_].ap[64:65, 0:qw], rhi[s_].ap[64:65, 0:qw], ALU.subtract, [rec[s_], rhi[s_]], [rlo[s_]])

                def f2(hi=hi, q0=q0, qw=qw, s_=s_):
                    pbc = psb[5]
                    ys = yst[s_]
                    mm(pbc.ap[0:64, 0:qw], ones.ap[64:65, 0:64], rhi[s_].ap[64:65, 0:qw], True, False, [ones, rhi[s_]], [pbc])
                    mm(pbc.ap[0:64, 0:qw], ones.ap[64:65, 0:64], rlo[s_].ap[64:65, 0:qw], False, True, [ones, rlo[s_]], [pbc])
                    tt(ys.ap[0:64, 0:qw], yraw[s_].ap[0:64, 0:qw], pbc.ap[0:64, 0:qw], ALU.mult, [yraw[s_], pbc], [ys])
                    dma("sp", YT_d[hi, :, q0:q0 + qw], ys.ap[0:64, 0:qw], [ys], [dt_["YT"]], "st_" + ys.name)
                f2p.append([DEF, s_, f2])

            def head_blocks(hi):
                i = hi % 2
                blks = []
                if hi < 8:
                    ctx_tiles = [(kt * 128, 128, 0, CTX, kt, None) for kt in range(2)]
                else:
                    ctx_tiles = [(kt * 64, 64, 0, CTX, kt, None) for kt in range(4)]
                if not last:
                    blks.append((0, CTX, ctx_tiles))
                for qb in range(4):
                    q0 = CTX + 512 * qb
                    if hi < 8:
                        tiles = [(kt * 128, 128, 0, 512, kt, None) for kt in range(NKT)]
                    else:
                        tiles = [(kt * 64, 64, 0, 512, kt, None) for kt in range(4)]
                        R = 8 * qb
                        for kr in range(32):
                            vr = na_valid_rows(kr, R)
                            if vr is None:
                                continue
                            qlo, qhi = vr
                            a0 = 7 + qlo - kr
                            nr = qhi - qlo
                            assert 0 <= a0 and a0 + nr <= 15
                            bias = TTb[i].ap[0:64, a0:a0 + nr, :].rearrange("p a q -> p (a q)")
                            tiles.append((CTX + 64 * kr, 64, (qlo - R) * 64, nr * 64, 4 + kr, bias))
                    blks.append((q0, 512, tiles))
                return blks

            pend = []
            blkc = [0]
            need_load = [None]

            def step_done():
                for x in f2p:
                    x[0] -= 1
                flush_f2()
                if need_load[0] is not None:
                    nh = need_load[0]
                    if all(e["hi"] != nh - 2 for e in pend):
                        load_head(nh)
                        need_load[0] = None

            load_head(0)
            for hi in range(16):
                if hi + 1 < 16:
                    need_load[0] = hi + 1
                if hi >= 8:
                    build_tt(hi)
                for (q0, qw, tiles) in head_blocks(hi):
                    acc = psb[6 + (blkc[0] % 2)]
                    set_ = blkc[0] % 2
                    blkc[0] += 1
                    nt_ = len(tiles)
                    for ti, tile in enumerate(tiles):
                        e = dict(hi=hi, tile=tile, q0=q0, qw=qw, acc=acc, set=set_, first=(ti == 0), last=(ti == nt_ - 1))
                        do_qk(e)
                        pend.append(e)
                        if len(pend) > LA:
                            do_pv(pend.pop(0))
                        step_done()
            while pend:
                do_pv(pend.pop(0))
                step_done()
            flush_f2(all_=True)
            assert need_load[0] is None
            P.barrier()

            AR.reset()
            hT = AR.b("hT", 128, [8, 512])
            ymn = [AR.b("ymn", 128, [8, 512]) for _ in range(2)]
            yc = AR.b("yc", 128, [8, 512])
            rt = AR.f("rt", 128, [512])
            rstd = AR.f("rstd", 128, [512])
            tmpf = [AR.f("tmpf", 128, [512]) for _ in range(2)]
            wg0 = rslot(0, 8, 1024)
            wg1 = rslot(1, 8, 1024)
            wo_v = rslot(3, 8, 1024)
            blocks = token_blocks(not last)

            def load_y(bi):
                t0, w, col = blocks[bi]
                y_ = ymn[bi % 2]
                for typ in range(2):
                    dma("sp", y_.ap[:, 4 * typ:4 * typ + 4, 0:w],
                        YT_d[8 * typ:8 * typ + 8, :, t0:t0 + w].rearrange("(j a) d t -> (a d) j t", a=2),
                        [dt_["YT"]], [y_], "ld_y%d" % (bi % 2))

            load_y(0)
            for bi, (t0, w, col) in enumerate(blocks):
                if bi + 1 < len(blocks):
                    load_y(bi + 1)
                y_ = ymn[bi % 2]
                norm_mod(hT, t0, w, col, A1, 0, rt, rstd, tmpf)
                for j in range(8):
                    pg1, pg2, pm1, pm2 = psum(), psum(), psum(), psum()
                    for kc in range(8):
                        mm(pg1.ap[:, 0:w], wg0[:, kc, j * 128:(j + 1) * 128], hT.ap[:, kc, 0:w], kc == 0, kc == 7, [ring[0], hT], [pg1])
                    for kc in range(8):
                        mm(pg2.ap[:, 0:w], wg1[:, kc, j * 128:(j + 1) * 128], hT.ap[:, kc, 0:w], kc == 0, kc == 7, [ring[1], hT], [pg2])
                    for kc in range(4):
                        mm(pm1.ap[:, 0:w], wmo_v[:, kc, j * 128:(j + 1) * 128], y_.ap[:, kc, 0:w], kc == 0, kc == 3, [ring[2], y_], [pm1])
                    for kc in range(4):
                        mm(pm2.ap[:, 0:w], wno_v[:, kc, j * 128:(j + 1) * 128], y_.ap[:, 4 + kc, 0:w], kc == 0, kc == 3, [ring[2], y_], [pm2])
                    s1, s2 = tmpf
                    act(s1.ap[:, 0:w], pg1.ap[:, 0:w], AF.Sigmoid, [pg1], [s1])
                    act(s2.ap[:, 0:w], pg2.ap[:, 0:w], AF.Sigmoid, [pg2], [s2])
                    tt(s1.ap[:, 0:w], s1.ap[:, 0:w], pm1.ap[:, 0:w], ALU.mult, [s1, pm1], [s1])
                    tt(s2.ap[:, 0:w], s2.ap[:, 0:w], pm2.ap[:, 0:w], ALU.mult, [s2, pm2], [s2])
                    tt(yc.ap[:, j, 0:w], s1.ap[:, 0:w], s2.ap[:, 0:w], ALU.add, [s1, s2], [yc])
                for i_ in range(8):
                    po = psum()
                    for j in range(8):
                        mm(po.ap[:, 0:w], wo_v[:, j, i_ * 128:(i_ + 1) * 128], yc.ap[:, j, 0:w], j == 0, j == 7, [ring[3], yc], [po])
                    stt(xT.ap[:, i_, t0:t0 + w], po.ap[:, 0:w], mod.ap[:, 16 + i_, col:col + 1], xT.ap[:, i_, t0:t0 + w],
                        ALU.mult, ALU.add, [po, mod, xT], [xT])
            P.barrier()
            if debug:
                dma("sp", xdbg_d[0].rearrange("(k p) t -> p k t", p=128), xT.ap[:, :, :], [xT], [dt_["xdbg"]], "dbg0")

            AR.reset()
            h2T = AR.b("h2T", 128, [8, 512])
            hid = AR.b("hid", 128, [32, 512])
            rt = AR.f("rt", 128, [512])
            rstd = AR.f("rstd", 128, [512])
            tmpf = [AR.f("tmpf", 128, [512]) for _ in range(2)]
            w1_v = w_ff1_d[l].rearrange("(k p) n -> p k n", p=128)
            w2_v = w_ff2_d[l].rearrange("(k p) n -> p k n", p=128)
            sl = [0]
            for bi, (t0, w, col) in enumerate(blocks):
                norm_mod(h2T, t0, w, col, A2, 24, rt, rstd, tmpf)
                for g in range(4):
                    s = sl[0] % 4
                    sl[0] += 1
                    wv = rslot(s, 8, 1024)
                    wload(s, wv, w1_v[:, :, g * 1024:(g + 1) * 1024])
                    for hc_ in range(8):
                        pt = psum()
                        for kc in range(8):
                            mm(pt.ap[:, 0:w], wv[:, kc, hc_ * 128:(hc_ + 1) * 128], h2T.ap[:, kc, 0:w], kc == 0, kc == 7, [ring[s], h2T], [pt])
                        tm = tmpf[hc_ % 2]
                        act(tm.ap[:, 0:w], pt.ap[:, 0:w], AF.Relu, [pt], [tm])
                        tt(hid.ap[:, g * 8 + hc_, 0:w], tm.ap[:, 0:w], tm.ap[:, 0:w], ALU.mult, [tm], [hid])
                for oc in range(4):
                    s = sl[0] % 4
                    sl[0] += 1
                    wv = rslot(s, 32, 256)
                    wload(s, wv, w2_v[:, :, oc * 256:(oc + 1) * 256])
                    for i2 in range(2):
                        i_ = oc * 2 + i2
                        po = psum()
                        for hc_ in range(32):
                            mm(po.ap[:, 0:w], wv[:, hc_, i2 * 128:(i2 + 1) * 128], hid.ap[:, hc_, 0:w], hc_ == 0, hc_ == 31, [ring[s], hid], [po])
                        stt(xT.ap[:, i_, t0:t0 + w], po.ap[:, 0:w], mod.ap[:, 40 + i_, col:col + 1], xT.ap[:, i_, t0:t0 + w],
                            ALU.mult, ALU.add, [po, mod, xT], [xT])
            P.barrier()
            if debug:
                dma("sp", xdbg_d[1].rearrange("(k p) t -> p k t", p=128), xT.ap[:, :, :], [xT], [dt_["xdbg"]], "dbg1")

        dma("sp", outT_d.rearrange("(k p) t -> p k t", p=128), xT.ap[:, :, CTX:T], [xT], [dt_["out"]], "outst")
        P.op("sp", None, reads=[dt_["out"], dt_["xdbg"]])
        P.emit()
    return nc


def _consts():
    t = np.arange(SEQ)
    inv = (10000.0 ** (-np.arange(8, dtype=np.float32) / 8.0)).astype(np.float32)
    ang_r = (t // 64).astype(np.float32)[:, None] * inv[None, :]
    ang_c = (t % 64).astype(np.float32)[:, None] * inv[None, :]
    cosT = np.ones((96, T), np.float32)
    sinT = np.zeros((96, T), np.float32)
    for i in range(8):
        for base, ang in ((64, ang_r), (80, ang_c)):
            cosT[base + i, CTX:] = np.cos(ang[:, i])
            cosT[base + 8 + i, CTX:] = np.cos(ang[:, i])
            sinT[base + i, CTX:] = np.sin(ang[:, i])
            sinT[base + 8 + i, CTX:] = np.sin(ang[:, i])
    rotT = np.zeros((96, 96), np.float32)
    for base in (64, 80):
        for i in range(8):
            rotT[base + 8 + i, base + i] = -1.0
            rotT[base + i, base + 8 + i] = 1.0
    dpad = np.zeros((31, 127), np.float32)
    for b in range(31):
        dpad[b, b + 48] = 1.0
    cm = np.full((64, 64), NEG, np.float32)
    for qc in range(64):
        cs = min(max(qc - 8, 0), 48)
        cm[cs:cs + 16, qc] = 0.0
    ident = np.eye(128, dtype=np.float32)
    bdiag = np.zeros((128, 128), np.float32)
    bdiag[0:64, 0:64] = 1.0
    bdiag[64:128, 64:128] = 1.0
    return dict(cosT=cosT, sinT=sinT, rotT=rotT, dpad=dpad, colmask=cm, ident=ident, bdiag=bdiag)


def _fm(v, nch):
    Lh = v.shape[0]
    return np.ascontiguousarray(v.reshape(Lh, nch, 128).transpose(0, 2, 1))


def make_in_maps(inp, depth=DEPTH):
    f = lambda a: np.ascontiguousarray(np.asarray(a, dtype=np.float32))
    L = depth
    shared = dict(
        w_ada=f(inp["w_ada"][:L]), b_adaT=_fm(f(inp["b_ada"][:L]), 48), g_attnT=_fm(f(inp["g_attn"][:L]), 8),
        w_in=f(inp["w_in"][:L]), g_qaT=_fm(f(inp["g_qa"][:L]), 6), w_uq=f(inp["w_uq"][:L]),
        g_kvaT=_fm(f(inp["g_kva"][:L]), 2), w_ukv=f(inp["w_ukv"][:L]),
        g_mq=f(inp["g_mla_q"][:L]).reshape(L, 96, 1), g_mk=f(inp["g_mla_k"][:L]).reshape(L, 96, 1),
        g_nq2=np.ascontiguousarray(np.tile(f(inp["g_na_q"][:L]), (1, 2)).reshape(L, 128, 1)),
        g_nk2=np.ascontiguousarray(np.tile(f(inp["g_na_k"][:L]), (1, 2)).reshape(L, 128, 1)),
        rpbT=np.ascontiguousarray(f(inp["rpb"][:L])[:, :, ::-1, :].transpose(0, 3, 1, 2).reshape(L, 31, 120)),
        w_mla_o=f(inp["w_mla_o"][:L]), w_na_o=f(inp["w_na_o"][:L]), w_out=f(inp["w_out"][:L]),
        g_mlpT=_fm(f(inp["g_mlp"][:L]), 8), w_ff1=f(inp["w_ff1"][:L]), w_ff2=f(inp["w_ff2"][:L]),
    )
    shared.update(_consts())
    x = f(inp["x"])
    ctx = f(inp["ctx"])
    c = f(inp["c"])
    c_ctx = f(inp["c_ctx"])
    maps = []
    for b in range(x.shape[0]):
        m = dict(shared)
        m["xT"] = np.ascontiguousarray(np.concatenate([ctx[b], x[b]], axis=0).T)
        cT = np.stack([c[b], c_ctx], axis=1).reshape(8, 128, 2).transpose(1, 0, 2).reshape(128, 16)
        m["cT"] = np.ascontiguousarray(cT)
        maps.append(m)
    return maps


_NC_CACHE = {}


def kernel(**inputs):
    if "full" not in _NC_CACHE:
        _NC_CACHE["full"] = build_program(DEPTH, False)
    nc = _NC_CACHE["full"]
    maps = make_in_maps(inputs, DEPTH)
    res = run_bass_kernel_spmd(nc, maps, core_ids=list(range(8)))
    out = np.stack([np.asarray(r["outT"]).T for r in res.results], axis=0)
    return np.ascontiguousarray(out.astype(np.float32))
```
